# Optimizing a Trainium2 kernel written in Bass

```python
import jax, jax.numpy as jnp
from jax import lax
import numpy as np

D_MODEL = 1024
BATCH = 8
SEQ = 4096
DEPTH = 4

HEAD_DIM = 64
MIX_WIDTH = D_MODEL
ATT_Q_HEADS = (MIX_WIDTH // 2) // HEAD_DIM
ATT_KV_HEADS = 2
ATT_GROUP = ATT_Q_HEADS // ATT_KV_HEADS
ATT_WIDTH = ATT_Q_HEADS * HEAD_DIM
ATT_KV_WIDTH = ATT_KV_HEADS * HEAD_DIM
WINDOW = 128
BLOCK = 128
RWKV_HEADS = (MIX_WIDTH // 4) // HEAD_DIM
RWKV_WIDTH = RWKV_HEADS * HEAD_DIM
DECAY_LORA = 64
ICLR_LORA = 64
GATE_LORA = 128
RWKV_GN_EPS = 64e-5
CONV_WIDTH = MIX_WIDTH - ATT_WIDTH - RWKV_WIDTH
CONV_K = 3
ATT_PROJ_WIDTH = ATT_WIDTH + 2 * ATT_KV_WIDTH
RWKV_SIZES = (RWKV_WIDTH, RWKV_WIDTH, RWKV_WIDTH, DECAY_LORA, ICLR_LORA, GATE_LORA)
RWKV_PROJ_WIDTH = 3 * RWKV_WIDTH + DECAY_LORA + ICLR_LORA + GATE_LORA
CONV_PROJ_WIDTH = 3 * CONV_WIDTH
PROJ_WIDTH = ATT_PROJ_WIDTH + RWKV_PROJ_WIDTH + CONV_PROJ_WIDTH
D_FF = 2816
N_MOD = 9
EPS = 1e-6

kernel_name = "hymba_style_hybrid_swa_rwkv7_shortconv_macaron_adaln"


def rms_norm(x, gain):
    xf = x.astype(jnp.float32)
    y = xf * lax.rsqrt(jnp.mean(xf * xf, axis=-1, keepdims=True) + EPS)
    return (y * gain.astype(jnp.float32)).astype(x.dtype)


def swiglu(x, w_in, w_out):
    gate, up = jnp.split(x @ w_in, 2, axis=-1)
    return (jax.nn.silu(gate) * up) @ w_out


def token_shift(p):
    return jnp.pad(p, ((0, 0), (1, 0), (0, 0)))[:, :-1]


def sliding_window_sink_attention(qkv, q_gain, k_gain, sinks):
    B, T, _ = qkv.shape
    nb = T // BLOCK
    q, k, v = jnp.split(qkv, [ATT_WIDTH, ATT_WIDTH + ATT_KV_WIDTH], axis=-1)
    q = rms_norm(q.reshape(B, T, ATT_Q_HEADS, HEAD_DIM), q_gain)
    k = rms_norm(k.reshape(B, T, ATT_KV_HEADS, HEAD_DIM), k_gain)
    v = v.reshape(B, T, ATT_KV_HEADS, HEAD_DIM)
    qb = q.reshape(B, nb, BLOCK, ATT_KV_HEADS, ATT_GROUP, HEAD_DIM)

    def band(z):
        zb = z.reshape(B, nb, BLOCK, ATT_KV_HEADS, HEAD_DIM)
        prev = jnp.pad(zb, ((0, 0), (1, 0), (0, 0), (0, 0), (0, 0)))[:, :-1]
        return jnp.concatenate([prev, zb], axis=2)

    kb, vb = band(k), band(v)
    scores = jnp.einsum('bnqhgd,bnkhd->bnhgqk', qb, kb).astype(jnp.float32) * (HEAD_DIM ** -0.5)
    blk = jnp.arange(nb)[:, None, None] * BLOCK
    q_pos = blk + jnp.arange(BLOCK)[None, :, None]
    k_pos = blk - BLOCK + jnp.arange(2 * BLOCK)[None, None, :]
    diff = q_pos - k_pos
    mask = (k_pos >= 0) & (diff >= 0) & (diff < WINDOW)
    scores = jnp.where(mask[None, :, None, None], scores, -jnp.inf)
    sink = sinks.astype(jnp.float32).reshape(ATT_KV_HEADS, ATT_GROUP)[None, None, :, :, None, None]
    sink = jnp.broadcast_to(sink, scores.shape[:-1] + (1,))
    probs = jax.nn.softmax(jnp.concatenate([scores, sink], axis=-1), axis=-1)[..., :-1]
    out = jnp.einsum('bnhgqk,bnkhd->bnqhgd', probs.astype(v.dtype), vb)
    return out.reshape(B, T, ATT_WIDTH)


def rwkv7_time_mix(p, mu, w0, w_w2, a0, w_a2, w_g2, k_k, k_a, r_k, gn_w, gn_b):
    B, T, _ = p.shape
    f32 = jnp.float32
    p = p + mu * (token_shift(p) - p)
    r, k, v, wd, ad, gd = jnp.split(p, np.cumsum(RWKV_SIZES)[:-1].tolist(), axis=-1)
    r, k, v = r.astype(f32), k.astype(f32), v.astype(f32)
    w_log = -jax.nn.softplus(-(w0 + jnp.tanh(wd) @ w_w2).astype(f32)) - 0.5
    decay = jnp.exp(-jnp.exp(w_log))
    a = jax.nn.sigmoid((a0 + ad @ w_a2).astype(f32))
    g = (jax.nn.sigmoid(gd) @ w_g2).astype(f32)
    kk = (k * k_k).reshape(B, T, RWKV_HEADS, HEAD_DIM)
    kk = kk / jnp.maximum(jnp.linalg.norm(kk, axis=-1, keepdims=True), 1e-12)
    k = k * (1.0 + (a - 1.0) * k_a)
    hs = lambda z: z.reshape(B, T, RWKV_HEADS, HEAD_DIM)
    r, k, v, a, decay = hs(r), hs(k), hs(v), hs(a), hs(decay)

    def step(S, inp):
        r_t, w_t, k_t, v_t, kk_t, a_t = inp
        s_kk = jnp.einsum('bhvk,bhk->bhv', S, kk_t)
        S = (S * w_t[:, :, None, :] - s_kk[..., None] * (kk_t * a_t)[:, :, None, :]
             + v_t[..., None] * k_t[:, :, None, :])
        return S, jnp.einsum('bhvk,bhk->bhv', S, r_t)

    xs = tuple(jnp.swapaxes(z, 0, 1) for z in (r, decay, k, v, kk, a))
    S0 = jnp.zeros((B, RWKV_HEADS, HEAD_DIM, HEAD_DIM), f32)
    _, y = lax.scan(step, S0, xs)
    y = jnp.swapaxes(y, 0, 1)
    mean = jnp.mean(y, axis=-1, keepdims=True)
    var = jnp.mean(jnp.square(y - mean), axis=-1, keepdims=True)
    y = ((y - mean) * lax.rsqrt(var + RWKV_GN_EPS)).reshape(B, T, RWKV_WIDTH) * gn_w + gn_b
    bonus = jnp.sum(r * k * r_k, axis=-1, keepdims=True) * v
    y = (y + bonus.reshape(B, T, RWKV_WIDTH)) * g
    return y.astype(p.dtype)


def short_conv_mix(p, conv_w):
    b_gate, c_gate, h = jnp.split(p, [CONV_WIDTH, 2 * CONV_WIDTH], axis=-1)
    u = c_gate * h
    y = lax.conv_general_dilated(u, conv_w[:, None, :].astype(u.dtype), window_strides=(1,),
                                 padding=[(CONV_K - 1, 0)], dimension_numbers=('NWC', 'WIO', 'NWC'),
                                 feature_group_count=CONV_WIDTH)
    return b_gate * y


def setup_inputs(seed: int = 0) -> dict:
    key = jax.random.key(seed)
    ks = iter(jax.random.split(key, 32))
    nrm = lambda shape, s: jax.random.normal(next(ks), shape, jnp.float32) * s
    L, D = DEPTH, D_MODEL
    return {
        'x': nrm((BATCH, SEQ, D), 1.0),
        'c': nrm((BATCH, D), 1.0),
        'w_ada': nrm((L, D, N_MOD * D), 0.1 * D ** -0.5),
        'b_ada': nrm((L, N_MOD * D), 0.01),
        'g_ffn1': 1.0 + nrm((L, D), 0.05),
        'w_ffn1_in': nrm((L, D, 2 * D_FF), D ** -0.5),
        'w_ffn1_out': nrm((L, D_FF, D), D_FF ** -0.5),
        'g_mix': 1.0 + nrm((L, D), 0.05),
        'w_mix_in': nrm((L, D, PROJ_WIDTH), D ** -0.5),
        'w_mix_out': nrm((L, MIX_WIDTH, D), MIX_WIDTH ** -0.5),
        'att_q_gain': 1.0 + nrm((L, HEAD_DIM), 0.05),
        'att_k_gain': 1.0 + nrm((L, HEAD_DIM), 0.05),
        'att_sinks': nrm((L, ATT_Q_HEADS), 1.0),
        'rwkv_mu': jax.random.uniform(next(ks), (L, RWKV_PROJ_WIDTH), jnp.float32),
        'rwkv_w0': jax.random.uniform(next(ks), (L, RWKV_WIDTH), jnp.float32, -6.0, -1.0),
        'rwkv_w_w2': nrm((L, DECAY_LORA, RWKV_WIDTH), 0.1 * DECAY_LORA ** -0.5),
        'rwkv_a0': nrm((L, RWKV_WIDTH), 0.1),
        'rwkv_a_w2': nrm((L, ICLR_LORA, RWKV_WIDTH), 0.1 * ICLR_LORA ** -0.5),
        'rwkv_g_w2': nrm((L, GATE_LORA, RWKV_WIDTH), GATE_LORA ** -0.5),
        'rwkv_k_k': 1.0 + nrm((L, RWKV_WIDTH), 0.1),
        'rwkv_k_a': 1.0 + nrm((L, RWKV_WIDTH), 0.1),
        'rwkv_r_k': nrm((L, RWKV_HEADS, HEAD_DIM), 0.1),
        'rwkv_gn_w': 1.0 + nrm((L, RWKV_WIDTH), 0.05),
        'rwkv_gn_b': nrm((L, RWKV_WIDTH), 0.01),
        'conv_w': nrm((L, CONV_K, CONV_WIDTH), CONV_K ** -0.5),
        'g_ffn2': 1.0 + nrm((L, D), 0.05),
        'w_ffn2_in': nrm((L, D, 2 * D_FF), D ** -0.5),
        'w_ffn2_out': nrm((L, D_FF, D), D_FF ** -0.5),
    }


def reference(x, c, w_ada, b_ada, g_ffn1, w_ffn1_in, w_ffn1_out, g_mix, w_mix_in, w_mix_out,
              att_q_gain, att_k_gain, att_sinks, rwkv_mu, rwkv_w0, rwkv_w_w2, rwkv_a0, rwkv_a_w2,
              rwkv_g_w2, rwkv_k_k, rwkv_k_a, rwkv_r_k, rwkv_gn_w, rwkv_gn_b, conv_w,
              g_ffn2, w_ffn2_in, w_ffn2_out):
    h = x
    B = x.shape[0]
    c_act = jax.nn.silu(c)
    for l in range(DEPTH):
        mod = (c_act @ w_ada[l] + b_ada[l]).reshape(B, N_MOD, D_MODEL)
        sh1, sc1, gt1, sh2, sc2, gt2, sh3, sc3, gt3 = [mod[:, i, None, :] for i in range(N_MOD)]
        hn = rms_norm(h, g_ffn1[l]) * (1.0 + sc1) + sh1
        h = h + 0.5 * (1.0 + gt1) * swiglu(hn, w_ffn1_in[l], w_ffn1_out[l])
        hn = rms_norm(h, g_mix[l]) * (1.0 + sc2) + sh2
        proj = hn @ w_mix_in[l]
        p_att, p_rwkv, p_conv = jnp.split(proj, [ATT_PROJ_WIDTH, ATT_PROJ_WIDTH + RWKV_PROJ_WIDTH], axis=-1)
        y_att = sliding_window_sink_attention(p_att, att_q_gain[l], att_k_gain[l], att_sinks[l])
        y_rwkv = rwkv7_time_mix(p_rwkv, rwkv_mu[l], rwkv_w0[l], rwkv_w_w2[l], rwkv_a0[l], rwkv_a_w2[l],
                                rwkv_g_w2[l], rwkv_k_k[l], rwkv_k_a[l], rwkv_r_k[l], rwkv_gn_w[l], rwkv_gn_b[l])
        y_conv = short_conv_mix(p_conv, conv_w[l])
        mixed = jnp.concatenate([y_att, y_rwkv, y_conv], axis=-1) @ w_mix_out[l]
        h = h + (1.0 + gt2) * mixed
        hn = rms_norm(h, g_ffn2[l]) * (1.0 + sc3) + sh3
        h = h + 0.5 * (1.0 + gt3) * swiglu(hn, w_ffn2_in[l], w_ffn2_out[l])
    return h
```

```python
import numpy as np
from contextlib import ExitStack
import concourse.bass as bass
import concourse.mybir as mybir
from concourse.bass_utils import run_bass_kernel_spmd

F32 = mybir.dt.float32
BF16 = mybir.dt.bfloat16
ALU = mybir.AluOpType
AF = mybir.ActivationFunctionType

D = 1024
DFF = 2816
NJ = 22
NT = 1024
NS = 512
CH = 64
NV = 160
C0 = 0.6065306597126334
EPS = 1e-6
GN_EPS = 64e-5

import os
PARTS = set(os.environ.get("MK_PARTS", "ffn1,att,rwkv,conv,ffn2").split(","))
ENGS = ("pe", "act", "dve", "pool", "sp")
SEM_ROT = 12000
SAME_ENGINE_SYNC = True


class Op:
    __slots__ = ("eng", "fn", "deps", "idx", "marked", "cnt", "is_dma", "slot", "dval", "semi")

    def __init__(self, eng, fn):
        self.eng = eng
        self.fn = fn
        self.deps = set()
        self.marked = False
        self.cnt = None
        self.is_dma = False
        self.slot = None
        self.dval = None
        self.semi = 0


class Prog:
    def __init__(self, nc):
        self.nc = nc
        self.ops = {e: [] for e in ENGS}
        self.last_w = {}
        self.readers = {}
        self.slot_cnt = {}
        self.out_dmas = []
        self.bar = {}

    def _add(self, o, reads, writes):
        deps = o.deps
        for k in reads:
            w = self.last_w.get(k)
            if w is not None:
                deps.add(w)
        for k in writes:
            w = self.last_w.get(k)
            if w is not None:
                deps.add(w)
            for r in self.readers.get(k, ()):
                deps.add(r)
        b = self.bar.pop(o.eng, None)
        if b:
            deps.update(b)
        deps.discard(o)
        for k in reads:
            self.readers.setdefault(k, []).append(o)
        for k in writes:
            self.last_w[k] = o
            self.readers[k] = []
        o.idx = len(self.ops[o.eng])
        self.ops[o.eng].append(o)
        return o

    def op(self, eng, fn, reads=(), writes=()):
        return self._add(Op(eng, fn), reads, writes)

    def dma(self, eng, fn, slot, reads=(), writes=(), is_out=False):
        o = Op(eng, fn)
        o.is_dma = True
        o.slot = slot
        self.slot_cnt[slot] = self.slot_cnt.get(slot, 0) + 16
        o.dval = self.slot_cnt[slot]
        self._add(o, reads, writes)
        if is_out:
            self.out_dmas.append(o)
        return o

    def barrier(self):
        last = []
        for e in ("pe", "act", "dve"):
            for o in reversed(self.ops[e]):
                if not o.is_dma:
                    last.append(o)
                    break
        for e in ENGS:
            if e != "pool":
                self.bar[e] = list(last)

    def emit(self):
        nc = self.nc
        for e in ENGS:
            for o in self.ops[e]:
                for d in o.deps:
                    if not d.is_dma:
                        if d.eng == o.eng and (d.eng == "pe" or not SAME_ENGINE_SYNC):
                            continue
                        d.marked = True
        nsem = {}
        for e in ENGS:
            c = 0
            semi = 0
            for o in self.ops[e]:
                if o.is_dma:
                    continue
                if o.marked:
                    if c >= SEM_ROT:
                        semi += 1
                        c = 0
                    c += 1
                    o.cnt = c
                    o.semi = semi
            nsem[e] = semi + 1
        slots = sorted(self.slot_cnt.keys(), key=str)
        with ExitStack() as es:
            esem = {e: [es.enter_context(nc.semaphore(f"s_{e}_{i}")) for i in range(nsem[e])] for e in ENGS}
            dsem = {s: es.enter_context(nc.semaphore(f"d_{i}")) for i, s in enumerate(slots)}
            block = es.enter_context(nc.Block())
            engmap = {"pe": block.tensor, "act": block.scalar, "dve": block.vector,
                      "pool": block.gpsimd, "sp": block.sync}

            def make(e):
                def body(eng):
                    waited = {}
                    for o in self.ops[e]:
                        for d in sorted(o.deps, key=lambda d: (d.eng, d.idx)):
                            if d.is_dma:
                                key = ("d", d.slot)
                                val = d.dval
                                sem = dsem[d.slot]
                            else:
                                if d.eng == e and (e == "pe" or not SAME_ENGINE_SYNC):
                                    continue
                                key = ("e", d.eng, d.semi)
                                val = d.cnt
                                sem = esem[d.eng][d.semi]
                            if waited.get(key, 0) >= val:
                                continue
                            waited[key] = val
                            eng.wait_ge(sem, val)
                        ins = o.fn(eng)
                        if o.is_dma:
                            ins.then_inc(dsem[o.slot], 16)
                        elif o.marked:
                            ins.then_inc(esem[e][o.semi], 1)
                    if e == "sp":
                        for o in self.out_dmas:
                            eng.wait_ge(dsem[o.slot], self.slot_cnt[o.slot])
                return body

            for e in ENGS:
                engmap[e](make(e))
        return nc


MIX_BLOCKS = None


def _mix_cols():
    blocks = []
    for i in range(4):
        blocks.append(np.arange(i * 128, (i + 1) * 128))
    blocks.append(np.arange(512, 640))
    blocks.append(np.concatenate([np.arange(576, 640), np.arange(512, 576)]))
    blocks.append(np.concatenate([np.arange(640, 704), np.arange(640, 704)]))
    blocks.append(np.concatenate([np.arange(704, 768), np.arange(704, 768)]))
    for c in range(6, 20):
        blocks.append(np.arange(c * 128, (c + 1) * 128))
    return np.concatenate(blocks)


def host_layout(inp, L):
    f = lambda a: np.ascontiguousarray(a, dtype=np.float32)
    out = {}

    def ffn_in(w):
        w = w.reshape(L, 8, 128, 2, NJ, 128)
        return f(w.transpose(0, 4, 2, 3, 1, 5).reshape(L, NJ, 128, 2048))

    def ffn_out(w):
        w = w.reshape(L, NJ, 128, 8, 128)
        return f(w.transpose(0, 3, 2, 1, 4).reshape(L, 8, 128, NJ * 128))

    out["w1i"] = ffn_in(inp["w_ffn1_in"][:L])
    out["w1o"] = ffn_out(inp["w_ffn1_out"][:L])
    out["w2i"] = ffn_in(inp["w_ffn2_in"][:L])
    out["w2o"] = ffn_out(inp["w_ffn2_out"][:L])
    wm = inp["w_mix_in"][:L][:, :, _mix_cols()]
    wm = wm.reshape(L, 8, 128, 11, 2, 128)
    out["wmi"] = f(wm.transpose(0, 3, 2, 4, 1, 5).reshape(L, 11, 128, 2048))
    wo = inp["w_mix_out"][:L]
    woz = np.zeros((L, 8, 128, 10, 128), np.float32)
    for s in range(4):
        woz[:, :, :, s, :] = wo[:, s * 128:(s + 1) * 128, :].reshape(L, 128, 8, 128).transpose(0, 2, 1, 3)
    for s in range(2):
        woz[:, :, :, 4 + s, :] = wo[:, 768 + s * 128:768 + (s + 1) * 128, :].reshape(L, 128, 8, 128).transpose(0, 2, 1, 3)
    for hd in range(4):
        woz[:, :, 0:64, 6 + hd, :] = wo[:, 512 + hd * 64:512 + (hd + 1) * 64, :].reshape(L, 64, 8, 128).transpose(0, 2, 1, 3)
    out["wmo"] = f(woz.reshape(L, 8, 128, 1280))
    wsm = np.zeros((L, 128, 3, 256), np.float32)
    wsm[:, 0:64, 0, :] = inp["rwkv_w_w2"][:L]
    wsm[:, 0:64, 1, :] = inp["rwkv_a_w2"][:L]
    wsm[:, :, 2, :] = inp["rwkv_g_w2"][:L]
    out["wsm"] = f(wsm)
    wa = inp["w_ada"][:L].reshape(L, 8, 128, 18, 512)
    out["wada"] = f(wa.transpose(0, 3, 2, 1, 4).reshape(L, 18, 128, 4096))
    vec = np.zeros((L, 128, NV), np.float32)
    fm = lambda v: v.reshape(L, -1, 128).transpose(0, 2, 1)
    vec[:, :, 0:8] = fm(inp["g_ffn1"][:L])
    vec[:, :, 8:16] = fm(inp["g_mix"][:L])
    vec[:, :, 16:24] = fm(inp["g_ffn2"][:L])
    vec[:, :, 24:96] = fm(inp["b_ada"][:L])
    vec[:, :, 96] = np.tile(inp["att_q_gain"][:L], (1, 2))
    vec[:, :, 97] = np.tile(inp["att_k_gain"][:L], (1, 2))
    vec[:, :, 98:106] = inp["att_sinks"][:L][:, None, :]
    mu = inp["rwkv_mu"][:L]
    hs = lambda v: v.reshape(L, 4, 64).transpose(0, 2, 1)
    for q in range(3):
        vec[:, 0:64, 106 + q * 4:110 + q * 4] = hs(mu[:, q * 256:(q + 1) * 256])
    vec[:, 0:64, 118] = mu[:, 768:832]
    vec[:, 0:64, 119] = mu[:, 832:896]
    vec[:, :, 120] = mu[:, 896:1024]
    vec[:, 0:64, 121:125] = hs(inp["rwkv_w0"][:L])
    vec[:, 0:64, 125:129] = hs(inp["rwkv_a0"][:L])
    vec[:, 0:64, 129:133] = hs(inp["rwkv_k_k"][:L])
    vec[:, 0:64, 133:137] = hs(inp["rwkv_k_a"][:L])
    vec[:, 0:64, 137:141] = hs(inp["rwkv_r_k"][:L].reshape(L, 256))
    vec[:, 0:64, 141:145] = hs(inp["rwkv_gn_w"][:L])
    vec[:, 0:64, 145:149] = hs(inp["rwkv_gn_b"][:L])
    cw = inp["conv_w"][:L]
    for c in range(2):
        for j in range(3):
            vec[:, :, 149 + c * 3 + j] = cw[:, j, c * 128:(c + 1) * 128]
    vec[:, 0:64, 155] = inp["att_q_gain"][:L]
    vec[:, 64:128, 156] = inp["att_q_gain"][:L]
    out["vec"] = f(vec.transpose(1, 0, 2))
    cst = np.zeros((128, 8, 128), np.float32)
    cst[:, 0] = np.eye(128)
    cst[:, 1] = 1.0
    cst[0:64, 2, 0:64] = 1.0
    cst[64:128, 2, 64:128] = 1.0
    cst[0:64, 3, 0:64] = 1.0 / 64.0
    jj = np.arange(128)[:, None]
    ii = np.arange(128)[None, :]
    cst[:, 4] = (jj <= ii)
    cst[:, 5] = (jj > ii)
    j6 = np.arange(64)[:, None]
    t6 = np.arange(64)[None, :]
    cst[0:64, 6, 0:64] = (t6 > j6)
    cst[0:64, 6, 64:128] = (t6 >= j6)
    cst[0:64, 7, 0:64] = (t6 < j6)
    cst[0:64, 7, 64:128] = (t6 < j6)
    out["cst"] = cst
    rm = np.ones((64, NS), np.float32)
    rm[:, ::CH] = 0.0
    out["rmask"] = rm
    return out


def build_program(L, T, debug=False):
    nc = bass.Bass("TRN2", target_bir_lowering=False)
    NP = T // NT
    dr = {}

    def din(name, shape):
        dr[name] = nc.dram_tensor(name, list(shape), F32, kind="ExternalInput").ap()
        return dr[name]

    xT = din("xT", (D, T))
    cT = din("cT", (128, 8))
    w1i = din("w1i", (L, NJ, 128, 2048))
    w1o = din("w1o", (L, 8, 128, NJ * 128))
    w2i = din("w2i", (L, NJ, 128, 2048))
    w2o = din("w2o", (L, 8, 128, NJ * 128))
    wmi = din("wmi", (L, 11, 128, 2048))
    wmo = din("wmo", (L, 8, 128, 1280))
    wsmd = din("wsm", (L, 128, 3, 256))
    wada = din("wada", (L, 18, 128, 4096))
    vecd = din("vec", (128, L, NV))
    cstd = din("cst", (128, 8, 128))
    rmd = din("rmask", (64, NS))
    outT = nc.dram_tensor("outT", [D, T], F32, kind="ExternalOutput").ap()

    P = Prog(nc)
    es = ExitStack()
    with es:
        def sb(name, shape, dt=F32):
            return es.enter_context(nc.sbuf_tensor("sb_" + name, list(shape), dt))

        h = sb("h", (128, 8, NT))
        hn = sb("hn", (128, 8, NT), BF16)
        NWS = 3
        wslot = [sb(f"wslot{i}", (128, 2816), BF16) for i in range(NWS)]
        vec = sb("vec", (128, L, NV))
        mods = sb("mods", (128, L, 72))
        coef = sb("coef", (128, L, 48))
        omka = sb("omka", (64, L, 4))
        esink = sb("esink", (128, L, 8))
        cact = sb("cact", (128, 8))
        cstb = sb("cstb", (128, 8, 128), BF16)
        rmask = sb("rmask", (64, NS), BF16)
        identrep = sb("identrep", (64, 8, 64), BF16)
        mrep = sb("mrep", (64, 2, 4, 128), BF16)
        amask = sb("amask", (128, 2, 512), BF16)
        wsm1 = sb("wsm", (128, 3, 256), BF16)
        kcar = sb("kcar", (128, L, 2, 128), BF16)
        vcar = sb("vcar", (128, L, 256), BF16)
        rwlast = sb("rwlast", (128, L, 15))
        ucar = sb("ucar", (128, L, 2, 2))
        Sst = sb("Sst", (64, L, 4, 64))
        hid = sb("hid", (128, NJ, NT), BF16)
        hidflat = hid[:, :, :].rearrange("p a b -> p (a b)")
        praw = hidflat[:, 0:15390].bitcast(F32).rearrange("p (a b) -> p a b", b=NS + 1)
        Xl = [hidflat[0:64, 15392 + i * 1024:15392 + (i + 1) * 1024].bitcast(F32) for i in range(6)]
        cstf = hidflat[:, 16384:18432].bitcast(F32).rearrange("p (a b) -> p a b", b=128)
        regA = sb("regA", (128, 8704), BF16)
        qn = regA[:, 0:4096].rearrange("p (a b) -> p a b", b=NS)
        kbuf = regA[:, 4096:5376].rearrange("p (a b) -> p a b", b=NS + 128)
        vbuf = regA[:, 5376:6656].rearrange("p (a b) -> p a b", b=256)
        PT = regA[:, 6656:8704].rearrange("p (a b c) -> p a b c", a=2, b=2)
        r8 = lambda lo, n: regA[0:64, lo:lo + 8 * n].rearrange("p (a b) -> p a b", b=n)
        E1 = r8(0, 128)
        E2 = r8(1024, 128)
        E3 = r8(2048, 128)
        An = [r8(3072, 64), r8(3584, 64)]
        AnT = [r8(4096, 64), r8(4608, 64)]
        Tn = [r8(5120, 64), r8(5632, 64)]
        Gm = r8(6144, 64)
        regB = sb("regB", (128, 2048))
        rb8 = lambda lo: regB[0:64, lo:lo + 512].rearrange("p (a b) -> p a b", b=64)
        Rp, Pc, Ov, Qc = rb8(0), rb8(512), rb8(1024), rb8(1536)
        ubuf = regB[:, 0:2 * (NS + 2)].rearrange("p (a b) -> p a b", b=NS + 2)
        ymix = sb("ymix", (128, 6, NS), BF16)
        yr = sb("yr", (64, 4, NS), BF16)
        t512 = [sb(f"t512_{i}", (128, NS)) for i in range(4)]
        b512 = [sb(f"b512_{i}", (128, NS), BF16) for i in range(3)]
        Xh = [sb(f"X{i}", (64, NS)) for i in range(6, 9)]
        X = Xl + [x[:, :] for x in Xh]
        AR = sb("AR", (64, 8, 2, 64), BF16)
        BK = sb("BK", (64, 8, 2, 64), BF16)
        Bh = sb("Bh", (64, NS), BF16)
        Kh = sb("Kh", (64, NS), BF16)
        Vb = sb("Vb", (64, NS), BF16)
        tok = sb("tok", (64, 8, 4, 64), BF16)
        Apt = sb("Apt", (64, 8, 64), BF16)
        Uvt = sb("Uvt", (64, 8, 64), BF16)
        gC = sb("gC", (64, 8))
        sgb = sb("sgb", (128, NS), BF16)
        twd = sb("twd", (64, NS), BF16)
        adb = sb("adb", (64, NS), BF16)
        ps = [es.enter_context(nc.psum_tensor(f"ps{i}", [128, 512], F32)) for i in range(8)]

        def mm(out, lhsT, rhs, start, stop, reads, writes):
            P.op("pe", lambda e: e.matmul(out, lhsT=lhsT, rhs=rhs, start=start, stop=stop), reads, writes)

        def act(out, in_, func, reads, writes, scale=None, bias=None):
            kw = {}
            if scale is not None:
                kw["scale"] = scale
            if bias is not None:
                kw["bias"] = bias
            P.op("act", lambda e: e.activation(out=out, in_=in_, func=func, **kw), reads, writes)

        def tt(out, in0, in1, op, reads, writes, eng="dve"):
            P.op(eng, lambda e: e.tensor_tensor(out=out, in0=in0, in1=in1, op=op), reads, writes)

        def ts(out, in0, s1, op0, reads, writes, s2=None, op1=None, eng="dve"):
            if op1 is None:
                P.op(eng, lambda e: e.tensor_scalar(out=out, in0=in0, scalar1=s1, scalar2=None, op0=op0), reads, writes)
            else:
                P.op(eng, lambda e: e.tensor_scalar(out=out, in0=in0, scalar1=s1, scalar2=s2, op0=op0, op1=op1), reads, writes)

        def stt(out, in0, scalar, in1, op0, op1, reads, writes):
            P.op("dve", lambda e: e.scalar_tensor_tensor(out=out, in0=in0, scalar=scalar, in1=in1, op0=op0, op1=op1), reads, writes)

        def rsqrt(out, in_, bias, reads, writes):
            act(out, in_, AF.Ln, reads, writes, bias=float(bias))
            act(out, out, AF.Exp, writes, writes, scale=-0.5)

        def cp(out, in_, reads, writes, eng="act"):
            if eng == "act":
                P.op("act", lambda e: e.copy(out=out, in_=in_), reads, writes)
            else:
                P.op(eng, lambda e: e.tensor_copy(out=out, in_=in_), reads, writes)

        wctr = [0]

        def wload(src, ncols):
            i = wctr[0] % NWS
            wctr[0] += 1
            dst = wslot[i][:, 0:ncols]
            P.dma("pool", lambda e: e.dma_start(out=dst, in_=src), ("w", i), writes=[("w", i)])
            return wslot[i], ("w", i)

        psc = [0]

        def nps():
            i = psc[0] % 8
            psc[0] += 1
            return ps[i], ("ps", i)

        P.dma("sp", lambda e: e.dma_start(out=vec[:], in_=vecd[:, :, :]), "vec", writes=["vec"])
        P.dma("sp", lambda e: e.dma_start(out=cstf[:], in_=cstd[:, :, :]), "cst", writes=["cstf"])
        P.dma("pool", lambda e: e.dma_start(out=rmask[:], in_=rmd[:, :]), "rmask", writes=["rmask"])
        P.dma("sp", lambda e: e.dma_start(out=cact[:], in_=cT[:, :]), "cact", writes=["cact0"])
        cp(cstb[:], cstf[:], ["cstf"], ["cstb"], eng="dve")
        ident = cstb[:, 0, :]
        ones = cstb[:, 1, :]
        bdones = cstb[:, 2, :]
        ones64s = cstb[0:64, 3, 0:64]
        ident64 = cstb[0:64, 0, 0:64]
        ident64f = cstf[0:64, 0, 0:64]
        for c8 in range(8):
            cp(identrep[:, c8, :], cstf[0:64, 0, 0:64], ["cstf"], ["identrep"], eng="dve")
        for c4 in range(4):
            cp(mrep[:, 0, c4, :], cstf[0:64, 6, :], ["cstf"], ["mrep"], eng="dve")
            cp(mrep[:, 1, c4, :], cstf[0:64, 7, :], ["cstf"], ["mrep"], eng="dve")
            cp(amask[:, 0, c4 * 128:(c4 + 1) * 128], cstf[:, 5, :], ["cstf"], ["amask"], eng="dve")
            cp(amask[:, 1, c4 * 128:(c4 + 1) * 128], cstf[:, 4, :], ["cstf"], ["amask"], eng="dve")
        act(cact[:], cact[:], AF.Silu, ["cact0"], ["cact"])
        for t_, nm in ((kcar, "kcar"), (vcar, "vcar"), (rwlast, "rwlast"), (ucar, "ucar"), (Sst, "S")):
            P.op("dve", lambda e, t_=t_: e.memset(t_[:], 0.0), (), [nm])
        adaf = [hidflat[:, 0:8192].bitcast(F32), hidflat[:, 8192:16384].bitcast(F32)]
        for l in range(L):
            pm, pk = nps()
            for blk in range(18):
                st = adaf[blk % 2]
                key = ("ada", blk % 2)
                P.dma("sp", lambda e, st=st, l=l, blk=blk: e.dma_start(out=st, in_=wada[l, blk]), key, writes=[key])
                stv = st.rearrange("p (k n) -> p k n", k=8)
                for oc in range(4):
                    col = blk * 4 + oc
                    for kc in range(8):
                        mm(pm[:, col:col + 1], stv[:, kc, oc * 128:(oc + 1) * 128], cact[:, kc:kc + 1],
                           kc == 0, kc == 7, [key, "cact"], [pk])
            tt(mods[:, l, :], pm[:, 0:72], vec[:, l, 24:96], ALU.add, [pk, "vec"], ["mods"])
            for i, (msc, mgt, gcol, half) in enumerate(((1, 2, 0, 0.5), (4, 5, 8, 1.0), (7, 8, 16, 0.5))):
                ts(coef[:, l, i * 16:i * 16 + 8], mods[:, l, msc * 8:msc * 8 + 8], 1.0, ALU.add, ["mods"], ["coef"], s2=32.0, op1=ALU.mult)
                tt(coef[:, l, i * 16:i * 16 + 8], coef[:, l, i * 16:i * 16 + 8], vec[:, l, gcol:gcol + 8], ALU.mult, ["coef", "vec"], ["coef"])
                ts(coef[:, l, i * 16 + 8:i * 16 + 16], mods[:, l, mgt * 8:mgt * 8 + 8], 1.0, ALU.add, ["mods"], ["coef"], s2=half, op1=ALU.mult)
            ts(omka[:, l, :], vec[0:64, l, 133:137], -1.0, ALU.mult, ["vec"], ["omka"], s2=1.0, op1=ALU.add)
            act(esink[:, l, :], vec[:, l, 98:106], AF.Exp, ["vec"], ["esink"])
        P.barrier()

        def rmsnorm(l, which, t0, n):
            acol = which * 16
            bcol = (0, 3, 6)[which] * 8
            for s in range(n // NS):
                c0 = t0 + s * NS
                pss, psk = nps()
                for kc in range(8):
                    sq = b512[kc % 2]
                    act(sq[:], h[:, kc, c0:c0 + NS], AF.Square, [("h", kc)], [("b512", kc % 2)])
                    mm(pss[:], ones, sq[:], kc == 0, kc == 7, [("b512", kc % 2), "cstb"], [psk])
                rs = t512[0]
                rsqrt(rs[:], pss[:], D * EPS, [psk], [("t512", 0)])
                for kc in range(8):
                    tmp = t512[1 + kc % 2]
                    tt(tmp[:], h[:, kc, c0:c0 + NS], rs[:], ALU.mult, [("h", kc), ("t512", 0)], [("t512", 1 + kc % 2)])
                    act(hn[:, kc, c0:c0 + NS], tmp[:], AF.Identity, [("t512", 1 + kc % 2), "coef", "mods"], [("hn", kc)],
                        scale=coef[:, l, acol + kc:acol + kc + 1], bias=mods[:, l, bcol + kc:bcol + kc + 1])

        def ffn(l, which, wi, wo):
            rmsnorm(l, which, 0, NT)
            ccol = which * 16 + 8
            for j in range(NJ):
                w, wk = wload(wi[l, j], 2048)
                wv = w[:, 0:2048].rearrange("p (t k n) -> p t k n", t=2, k=8)
                for s in range(2):
                    pg, pgk = nps()
                    pu, puk = nps()
                    for kc in range(8):
                        mm(pg[:], wv[:, 0, kc, :], hn[:, kc, s * NS:(s + 1) * NS], kc == 0, kc == 7, [wk, ("hn", kc)], [pgk])
                    for kc in range(8):
                        mm(pu[:], wv[:, 1, kc, :], hn[:, kc, s * NS:(s + 1) * NS], kc == 0, kc == 7, [wk, ("hn", kc)], [puk])
                    sg = t512[(j * 2 + s) % 2 + 1]
                    sgk = ("t512", (j * 2 + s) % 2 + 1)
                    act(sg[:], pg[:], AF.Silu, [pgk], [sgk])
                    tt(hid[:, j, s * NS:(s + 1) * NS], sg[:], pu[:], ALU.mult, [sgk, puk], [("hid", j, s)])
            for m in range(8):
                w, wk = wload(wo[l, m], NJ * 128)
                wv = w[:, 0:NJ * 128].rearrange("p (k n) -> p k n", k=NJ)
                for s in range(2):
                    po, pok = nps()
                    for j in range(NJ):
                        mm(po[:], wv[:, j, :], hid[:, j, s * NS:(s + 1) * NS], j == 0, j == NJ - 1, [wk, ("hid", j, s)], [pok])
                    stt(h[:, m, s * NS:(s + 1) * NS], po[:], coef[:, l, ccol + m:ccol + m + 1], h[:, m, s * NS:(s + 1) * NS],
                        ALU.mult, ALU.add, [pok, "coef", ("h", m)], [("h", m)])

        def mixer(l, st, first):
            t0 = st * NS
            P.barrier()
            if st == 0:
                P.dma("pool", lambda e: e.dma_start(out=wsm1[:], in_=wsmd[l]), "wsm", writes=["wsm"])
            rmsnorm(l, 1, t0, NS)
            hs = lambda kc: hn[:, kc, t0:t0 + NS]
            HN = [("hn", kc) for kc in range(8)]
            cp(kbuf[:, :, 0:128], kcar[:, l], ["kcar"], ["kbuf"])
            cp(vbuf[:, 0, :], vcar[:, l], ["vcar"], ["vbuf"])
            cp(praw[:, :, 0:1], rwlast[:, l, :].rearrange("p (a b) -> p a b", b=1), ["rwlast"], ["praw"])

            def qknorm(pp, ppk, gcol, dst, dkey, dst2=None):
                sq = b512[2]
                act(sq[:], pp[:], AF.Square, [ppk], [("b512", 2)])
                p2, p2k = nps()
                mm(p2[:], bdones, sq[:], True, True, [("b512", 2), "cstb"], [p2k])
                rs = t512[3]
                rsqrt(rs[:], p2[:], 64 * EPS, [p2k], [("t512", 3)])
                if dst2 is None:
                    stt(dst, pp[:], vec[:, l, gcol:gcol + 1], rs[:], ALU.mult, ALU.mult, [ppk, "vec", ("t512", 3)], [dkey])
                else:
                    stt(dst, pp[:], vec[:, l, 155:156], rs[:], ALU.mult, ALU.mult, [ppk, "vec", ("t512", 3)], [dkey])
                    stt(dst2, pp[:], vec[:, l, 156:157], rs[:], ALU.mult, ALU.mult, [ppk, "vec", ("t512", 3)], [dkey])

            def getw(i):
                w, wk = wload(wmi[l, i], 2048)
                v = w[:, 0:2048].rearrange("p (t k n) -> p t k n", t=2, k=8)
                return v, wk

            def pj(v, wk, t, m0, m1):
                pp, ppk = nps()
                npart = m1 - m0
                for kc in range(8):
                    mm(pp[0:npart, :], v[:, t, kc, m0:m1], hs(kc), kc == 0, kc == 7, [wk, HN[kc]], [ppk])
                return pp, ppk

            if "att" not in PARTS or "rwkv" not in PARTS or "conv" not in PARTS:
                P.op("dve", lambda e: e.memset(ymix[:], 0.0), (), ["ymix"])
                P.op("dve", lambda e: e.memset(yr[:], 0.0), (), [("yr", i) for i in range(4)])
            for i in range(2 if "att" in PARTS else 0):
                v, wk = getw(i)
                for t in range(2):
                    pp, ppk = pj(v, wk, t, 0, 128)
                    qknorm(pp, ppk, 96, qn[:, (i * 2 + t) * 2, :], "qn", dst2=qn[:, (i * 2 + t) * 2 + 1, :])
            v, wk = getw(2)
            for t in range(2 if "att" in PARTS else 0):
                pp, ppk = pj(v, wk, t, 0, 128)
                qknorm(pp, ppk, 97, kbuf[:, t, 128:128 + NS], "kbuf")
            v, wk = getw(3)
            for b in range(4 if "att" in PARTS else 0):
                pp, ppk = nps()
                for kc in range(8):
                    mm(pp[:, 0:128], hn[:, kc, t0 + b * 128:t0 + (b + 1) * 128], v[:, 0, kc, :], kc == 0, kc == 7, [wk, HN[kc]], [ppk])
                for kc in range(8):
                    mm(pp[:, 128:256], hn[:, kc, t0 + b * 128:t0 + (b + 1) * 128], v[:, 1, kc, :], kc == 0, kc == 7, [wk, HN[kc]], [ppk])
                cp(vbuf[:, 1 + b, :], pp[:, 0:256], [ppk], ["vbuf"])
            for b in range(4 if "att" in PARTS else 0):
                for g in range(2):
                    kbs = (1,) if (first and b == 0) else (0, 1)
                    pv, pvk = nps()
                    pd, pdk = nps()
                    pb = (b * 2 + g) % 2
                    for kb in kbs:
                        sp_, spk = nps()
                        for j in range(4):
                            hd = 4 * g + j
                            half = hd % 2
                            kt = 0 if g == half else 1
                            kcol = (b + kb) * 128
                            mm(sp_[:, j * 128:(j + 1) * 128],
                               kbuf[:, kt, kcol:kcol + 128],
                               qn[:, hd, b * 128:(b + 1) * 128],
                               True, True, ["kbuf", "qn"], [spk])
                        act(PT[:, pb, kb, :], sp_[:], AF.Exp, [spk], [("PT", pb, kb)], scale=8.0)
                        tt(PT[:, pb, kb, :], PT[:, pb, kb, :], amask[:, kb, :], ALU.mult, [("PT", pb, kb), "amask"], [("PT", pb, kb)])
                    for n_, kb in enumerate(kbs):
                        mm(pv[:], vbuf[:, b + kb, g * 128:(g + 1) * 128], PT[:, pb, kb, :], n_ == 0, n_ == len(kbs) - 1, ["vbuf", ("PT", pb, kb)], [pvk])
                    for n_, kb in enumerate(kbs):
                        mm(pd[:], ones, PT[:, pb, kb, :], n_ == 0, n_ == len(kbs) - 1, ["cstb", ("PT", pb, kb)], [pdk])
                    den = t512[3]
                    for j in range(4):
                        ts(den[:, j * 128:(j + 1) * 128], pd[:, j * 128:(j + 1) * 128], esink[:, l, 4 * g + j:4 * g + j + 1], ALU.add,
                           [pdk, "esink"], [("t512", 3)])
                    P.op("dve", lambda e, den=den: e.reciprocal(out=den[:], in_=den[:]), [("t512", 3)], [("t512", 3)])
                    for j in range(4):
                        hd = 4 * g + j
                        half = hd % 2
                        sl = slice(half * 64, (half + 1) * 64)
                        tt(ymix[sl, hd // 2, b * 128:(b + 1) * 128], pv[sl, j * 128:(j + 1) * 128], den[sl, j * 128:(j + 1) * 128],
                           ALU.mult, [pvk, ("t512", 3)], ["ymix"])
            cp(kcar[:, l], kbuf[:, :, NS:NS + 128], ["kbuf"], ["kcar"])
            cp(vcar[:, l], vbuf[:, 4, :], ["vbuf"], ["vcar"])
            P.barrier()

            for i in range(4, 8 if "rwkv" in PARTS else 4):
                v, wk = getw(i)
                for t in range(2):
                    cidx = 6 + (i - 4) * 2 + t
                    if cidx == 13:
                        pp, ppk = pj(v, wk, t, 0, 128)
                        cp(praw[:, 14, 1:NS + 1], pp[:], [ppk], ["praw"])
                    else:
                        for hf in range(2):
                            slot = (cidx - 6) * 2 + hf
                            pp, ppk = pj(v, wk, t, hf * 64, (hf + 1) * 64)
                            cp(praw[0:64, slot, 1:NS + 1], pp[0:64, :], [ppk], ["praw"])
            cp(rwlast[:, l, :].rearrange("p (a b) -> p a b", b=1), praw[:, :, NS:NS + 1], ["praw"], ["rwlast"])
            for slot in range(15 if "rwkv" in PARTS else 0):
                npt = 128 if slot == 14 else 64
                mcol = 106 + slot if slot < 12 else (118 + slot - 12)
                d = t512[slot % 2]
                dk = ("t512", slot % 2)
                tt(d[0:npt, :], praw[0:npt, slot, 0:NS], praw[0:npt, slot, 1:NS + 1], ALU.subtract, ["praw"], [dk])
                stt(praw[0:npt, slot, 1:NS + 1], d[0:npt, :], vec[0:npt, l, mcol:mcol + 1], praw[0:npt, slot, 1:NS + 1],
                    ALU.mult, ALU.add, [dk, "vec", "praw"], ["praw"])
            act(twd[:], praw[0:64, 12, 1:NS + 1], AF.Tanh, ["praw"], ["twd"])
            cp(adb[:], praw[0:64, 13, 1:NS + 1], ["praw"], ["adb"])
            act(sgb[:], praw[:, 14, 1:NS + 1], AF.Sigmoid, ["praw"], ["sgb"])

            c3 = lambda ap: ap.rearrange("p (c t) -> p c t", t=CH)
            for hd in range(4 if "rwkv" in PARTS else 0):
                r_ = praw[0:64, hd, 1:NS + 1]
                k_ = praw[0:64, 4 + hd, 1:NS + 1]
                v_ = praw[0:64, 8 + hd, 1:NS + 1]
                hcol = lambda base: vec[0:64, l, base + hd:base + hd + 1]
                xs, xcs, xa, xe1, xe2, xt, xkk, xkm, xbo = X
                XK = [("X", i) for i in range(9)]
                hsl = slice(hd * 64, (hd + 1) * 64)
                pz, pzk = nps()
                mm(pz[0:64, :], wsm1[0:64, 0, hsl], twd[:], True, True, ["wsm", "twd"], [pzk])
                act(xs, pz[0:64, :], AF.Sigmoid, [pzk, "vec"], [XK[0]], bias=hcol(121))
                pa, pak = nps()
                mm(pa[0:64, :], wsm1[0:64, 1, hsl], adb[:], True, True, ["wsm", "adb"], [pak])
                act(xa, pa[0:64, :], AF.Sigmoid, [pak, "vec"], [XK[2]], bias=hcol(125))
                P.op("dve", lambda e, xcs=xcs, xs=xs: e.tensor_tensor_scan(out=xcs, data0=rmask[:], data1=xs, initial=0.0, op0=ALU.mult, op1=ALU.add),
                     ["rmask", XK[0]], [XK[1]])
                act(xe1, xcs, AF.Exp, [XK[1]], [XK[3]], scale=-C0)
                tt(AR[:, :, 1, :], c3(r_), c3(xe1), ALU.mult, ["praw", XK[3]], ["AR1"])
                cp(gC[:], c3(xe1)[:, :, CH - 1], [XK[3]], ["gC"], eng="dve")
                act(xe2, xcs, AF.Exp, [XK[1]], [XK[4]], scale=C0)
                tt(xt, xcs, xs, ALU.subtract, [XK[1], XK[0]], [XK[5]])
                act(xe1, xt, AF.Exp, [XK[5]], [XK[3]], scale=-C0)
                ts(xkk, k_, hcol(129), ALU.mult, ["praw", "vec"], [XK[6]])
                act(b512[2][0:64, :], xkk, AF.Square, [XK[6]], [("b512", 2)])
                pn, pnk = nps()
                mm(pn[0:64, :], cstb[0:64, 1, 0:64], b512[2][0:64, :], True, True, [("b512", 2), "cstb"], [pnk])
                rsqrt(xt, pn[0:64, :], 1e-18, [pnk], [XK[5]])
                tt(xkk, xkk, xt, ALU.mult, [XK[6], XK[5]], [XK[6]])
                stt(AR[:, :, 0, :], c3(xkk), -1.0, c3(xe1), ALU.mult, ALU.mult, [XK[6], XK[3]], ["AR0"])
                tt(xt, xkk, xa, ALU.mult, [XK[6], XK[2]], [XK[5]])
                tt(BK[:, :, 0, :], c3(xt), c3(xe2), ALU.mult, [XK[5], XK[4]], ["BK0"])
                tt(c3(xe1), c3(xcs)[:, :, CH - 1:CH].to_broadcast([64, 8, CH]), c3(xcs), ALU.subtract, [XK[1], XK[3]], [XK[3]])
                act(xe1, xe1, AF.Exp, [XK[3]], [XK[3]], scale=-C0)
                tt(Bh[:], xt, xe1, ALU.mult, [XK[5], XK[3]], ["Bh"])
                ts(xkm, xa, hcol(133), ALU.mult, [XK[2], "vec", "omka"], [XK[7]], s2=omka[:, l, hd:hd + 1], op1=ALU.add)
                tt(xkm, xkm, k_, ALU.mult, [XK[7], "praw"], [XK[7]])
                tt(BK[:, :, 1, :], c3(xkm), c3(xe2), ALU.mult, [XK[7], XK[4]], ["BK1"])
                tt(Kh[:], xkm, xe1, ALU.mult, [XK[7], XK[3]], ["Kh"])
                cp(Vb[:], v_, ["praw"], ["Vb"])
                stt(b512[2][0:64, :], r_, hcol(137), xkm, ALU.mult, ALU.mult, ["praw", "vec", XK[7]], [("b512", 2)])
                pbn, pbk = nps()
                mm(pbn[0:64, :], cstb[0:64, 1, 0:64], b512[2][0:64, :], True, True, [("b512", 2), "cstb"], [pbk])
                tt(xbo, pbn[0:64, :], v_, ALU.mult, [pbk, "praw"], [XK[8]])
                for cp2 in range(4):
                    ptk, ptkk = nps()
                    for cc in range(2):
                        c = cp2 * 2 + cc
                        srcs = (AR[:, c, 0, :], Bh[:, c * CH:(c + 1) * CH], Kh[:, c * CH:(c + 1) * CH], Vb[:, c * CH:(c + 1) * CH])
                        for q, s_ in enumerate(srcs):
                            o_ = ptk[0:64, (cc * 4 + q) * 64:(cc * 4 + q + 1) * 64]
                            mm(o_, s_, ident64, True, True, ["AR0", "Bh", "Kh", "Vb", "cstb"], [ptkk])
                    cp(tok[:, cp2 * 2:cp2 * 2 + 2, :, :], ptk[0:64, :].rearrange("p (c q n) -> p c q n", c=2, q=4), [ptkk], ["tok"])
                for (E, ek, mi, lh, rh, lk, rk) in ((E1, "E1", 0, 0, "AR", "BK0", ("AR0", "AR1")), (E2, "E2", 0, 1, "AR", "BK1", ("AR0", "AR1")),
                                                     (E3, "E3", 1, 0, "BK", "AR0", ("BK0", "BK1"))):
                    for hb in range(2):
                        pe_, pek = nps()
                        for cc in range(4):
                            c = hb * 4 + cc
                            if rh == "AR":
                                lhs = BK[:, c, lh, :]
                                rhs = AR[:, c, :, :].rearrange("p a t -> p (a t)")
                            else:
                                lhs = AR[:, c, 0, :]
                                rhs = BK[:, c, :, :].rearrange("p a t -> p (a t)")
                            mm(pe_[0:64, cc * 128:(cc + 1) * 128], lhs, rhs, True, True, [lk, rk[0], rk[1]], [pek])
                        tt(E[:, hb * 4:(hb + 1) * 4, :], pe_[0:64, :].rearrange("p (c n) -> p c n", c=4), mrep[:, mi], ALU.mult, [pek, "mrep"], [ek])
                tt(Tn[0][:], E1[:, :, 0:64], identrep[:], ALU.add, ["E1", "identrep"], [("Tn", 0)])
                Anat, AT = E1[:, :, 0:64], E3[:, :, 0:64]
                Ak, ATk = "E1", "E3"
                nlev = 5
                for n in range(1, nlev + 1):
                    pi = n % 2
                    psq, psqk = nps()
                    for c in range(8):
                        mm(psq[0:64, c * 64:(c + 1) * 64], Anat[:, c, :], AT[:, c, :], True, True, [Ak, ATk], [psqk])
                    cp(AnT[pi][:], psq[0:64, :].rearrange("p (c n) -> p c n", c=8), [psqk], [("AnT", pi)])
                    if n < nlev:
                        psn, psnk = nps()
                        for c in range(8):
                            mm(psn[0:64, c * 64:(c + 1) * 64], AT[:, c, :], Anat[:, c, :], True, True, [Ak, ATk], [psnk])
                        cp(An[pi][:], psn[0:64, :].rearrange("p (c n) -> p c n", c=8), [psnk], [("An", pi)], eng="dve")
                    pt_, ptk_ = nps()
                    for c in range(8):
                        mm(pt_[0:64, c * 64:(c + 1) * 64], ident64, Tn[1 - pi][:, c, :], True, False, ["cstb", ("Tn", 1 - pi)], [ptk_])
                        mm(pt_[0:64, c * 64:(c + 1) * 64], AnT[pi][:, c, :], Tn[1 - pi][:, c, :], False, True, [("AnT", pi), ("Tn", 1 - pi)], [ptk_])
                    cp(Tn[pi][:], pt_[0:64, :].rearrange("p (c n) -> p c n", c=8), [ptk_], [("Tn", pi)], eng="dve")
                    Anat, AT = An[pi][:], AnT[pi][:]
                    Ak, ATk = ("An", pi), ("AnT", pi)
                Tf = Tn[nlev % 2]
                Tk = ("Tn", nlev % 2)
                v8 = lambda p_: p_[0:64, :].rearrange("p (c n) -> p c n", c=8)
                pG, pGk = nps()
                pA, pAk = nps()
                for c in range(8):
                    mm(pG[0:64, c * 64:(c + 1) * 64], E3[:, c, 64:128], Tf[:, c, :], True, True, ["E3", Tk], [pGk])
                    mm(pA[0:64, c * 64:(c + 1) * 64], Tf[:, c, :], tok[:, c, 0, :], True, True, [Tk, "tok"], [pAk])
                cp(Gm[:], v8(pG), [pGk], ["Gm"])
                cp(Apt[:], v8(pA), [pAk], ["Apt"], eng="dve")
                pU, pUk = nps()
                pR, pRk = nps()
                pP, pPk = nps()
                for c in range(8):
                    mm(pU[0:64, c * 64:(c + 1) * 64], Gm[:, c, :], tok[:, c, 3, :], True, True, ["Gm", "tok"], [pUk])
                    mm(pR[0:64, c * 64:(c + 1) * 64], Apt[:, c, :], E1[:, c, 64:128], True, True, ["Apt", "E1"], [pRk])
                    mm(pP[0:64, c * 64:(c + 1) * 64], Apt[:, c, :], tok[:, c, 1, :], True, True, ["Apt", "tok"], [pPk])
                cp(Uvt[:], v8(pU), [pUk], ["Uvt"])
                tt(Rp[:], v8(pR), AR[:, :, 1, :], ALU.add, [pRk, "AR1"], ["Rp"])
                tt(Pc[:], identrep[:], gC[:].rearrange("p (c o) -> p c o", o=1).to_broadcast([64, 8, 64]), ALU.mult, ["identrep", "gC"], ["Pc"])
                tt(Pc[:], Pc[:], v8(pP), ALU.add, ["Pc", pPk], ["Pc"])
                pO, pOk = nps()
                pQ, pQk = nps()
                for c in range(8):
                    mm(pO[0:64, c * 64:(c + 1) * 64], Uvt[:, c, :], E1[:, c, 64:128], True, False, ["Uvt", "E1"], [pOk])
                    mm(pO[0:64, c * 64:(c + 1) * 64], tok[:, c, 3, :], E2[:, c, 64:128], False, True, ["tok", "E2"], [pOk])
                    mm(pQ[0:64, c * 64:(c + 1) * 64], tok[:, c, 1, :], Uvt[:, c, :], True, False, ["tok", "Uvt"], [pQk])
                    mm(pQ[0:64, c * 64:(c + 1) * 64], tok[:, c, 2, :], tok[:, c, 3, :], False, True, ["tok"], [pQk])
                cp(Ov[:], v8(pO), [pOk], ["Ov"])
                cp(Qc[:], v8(pQ), [pQk], ["Qc"], eng="dve")
                Y = xs
                for c in range(8):
                    Sa = Sst[:, l, hd, :]
                    Sb = Sst[:, l, hd, :]
                    ka, kb_ = ("S", l, hd), ("S", l, hd)
                    po_, pok_ = nps()
                    mm(po_[0:64, 0:64], Sa, Rp[:, c, :], True, True, [ka, "Rp"], [pok_])
                    mm(po_[0:64, 64:128], Pc[:, c, :], Sa, True, True, [ka, "Pc"], [pok_])
                    tt(Y[:, c * CH:(c + 1) * CH], po_[0:64, 0:64], Ov[:, c, :], ALU.add, [pok_, "Ov"], [XK[0]])
                    tt(Sb, po_[0:64, 64:128], Qc[:, c, :], ALU.add, [pok_, "Qc"], [kb_])
                YK = [XK[0]]
                cp(b512[2][0:64, :], Y, YK, [("b512", 2)])
                pm_, pmk = nps()
                mm(pm_[0:64, :], ones64s, b512[2][0:64, :], True, True, [("b512", 2), "cstb"], [pmk])
                tt(xcs, Y, pm_[0:64, :], ALU.subtract, YK + [pmk], [XK[1]])
                act(b512[2][0:64, :], xcs, AF.Square, [XK[1]], [("b512", 2)])
                pv_, pvk_ = nps()
                mm(pv_[0:64, :], ones64s, b512[2][0:64, :], True, True, [("b512", 2), "cstb"], [pvk_])
                rsqrt(xt, pv_[0:64, :], GN_EPS, [pvk_], [XK[5]])
                tt(xcs, xcs, xt, ALU.mult, [XK[1], XK[5]], [XK[1]])
                act(xcs, xcs, AF.Identity, [XK[1], "vec"], [XK[1]], scale=hcol(141), bias=hcol(145))
                tt(xcs, xcs, xbo, ALU.add, [XK[1], XK[8]], [XK[1]])
                pg_, pgk_ = nps()
                mm(pg_[0:64, :], wsm1[:, 2, hsl], sgb[:], True, True, ["wsm", "sgb"], [pgk_])
                tt(yr[:, hd, :], xcs, pg_[0:64, :], ALU.mult, [XK[1], pgk_], [("yr", hd)])

            P.barrier()
            cp(ubuf[:, :, 0:2], ucar[:, l], ["ucar"], ["ubuf"])
            vB, wkB = getw(8)
            Bsb = t512[0]
            vC, wkC = getw(9)
            vH, wkH = getw(10)
            for t in range(2 if "conv" in PARTS else 0):
                pB, pBk = pj(vB, wkB, t, 0, 128)
                pC, pCk = pj(vC, wkC, t, 0, 128)
                pH, pHk = pj(vH, wkH, t, 0, 128)
                cp(t512[0][:], pB[:], [pBk], [("t512", 0)])
                cp(t512[1][:], pC[:], [pCk], [("t512", 1)])
                tt(ubuf[:, t, 2:NS + 2], t512[1][:], pH[:], ALU.mult, [("t512", 1), pHk], ["ubuf"])
                cw = lambda j: vec[:, l, 149 + t * 3 + j:149 + t * 3 + j + 1]
                acc = t512[2]
                ts(acc[:], ubuf[:, t, 2:NS + 2], cw(2), ALU.mult, ["ubuf", "vec"], [("t512", 2)])
                stt(acc[:], ubuf[:, t, 1:NS + 1], cw(1), acc[:], ALU.mult, ALU.add, ["ubuf", "vec", ("t512", 2)], [("t512", 2)])
                stt(acc[:], ubuf[:, t, 0:NS], cw(0), acc[:], ALU.mult, ALU.add, ["ubuf", "vec", ("t512", 2)], [("t512", 2)])
                tt(ymix[:, 4 + t, :], acc[:], t512[0][:], ALU.mult, [("t512", 2), ("t512", 0)], ["ymix"])
            cp(ucar[:, l], ubuf[:, :, NS:NS + 2], ["ubuf"], ["ucar"])
            for m in range(8):
                w, wk = wload(wmo[l, m], 1280)
                wv = w[:, 0:1280].rearrange("p (s n) -> p s n", s=10)
                po, pok = nps()
                for s_ in range(6):
                    mm(po[:], wv[:, s_, :], ymix[:, s_, :], s_ == 0, False, [wk, "ymix"], [pok])
                for hd in range(4):
                    mm(po[:], wv[0:64, 6 + hd, :], yr[:, hd, :], False, hd == 3, [wk, ("yr", hd)], [pok])
                stt(h[:, m, t0:t0 + NS], po[:], coef[:, l, 24 + m:25 + m], h[:, m, t0:t0 + NS], ALU.mult, ALU.add,
                    [pok, "coef", ("h", m)], [("h", m)])

        xv = xT.rearrange("(c p) t -> p c t", p=128)
        ov = outT.rearrange("(c p) t -> p c t", p=128)
        for p_ in range(NP):
            tb = p_ * NT
            for c in range(8):
                P.dma("sp", lambda e, c=c, tb=tb: e.dma_start(out=h[:, c, :], in_=xv[:, c, tb:tb + NT]), ("hin", c), writes=[("h", c)])
            for l in range(L):
                if "ffn1" in PARTS:
                    ffn(l, 0, w1i, w1o)
                P.barrier()
                if PARTS & {"att", "rwkv", "conv", "mix"}:
                    for st in range(NT // NS):
                        mixer(l, st, first=(p_ == 0 and st == 0))
                P.barrier()
                if "ffn2" in PARTS:
                    ffn(l, 2, w2i, w2o)
                P.barrier()
            for c in range(8):
                P.dma("sp", lambda e, c=c, tb=tb: e.dma_start(out=ov[:, c, tb:tb + NT], in_=h[:, c, :]), ("hout", c), reads=[("h", c)], is_out=True)
        P.emit()
    return nc


_CACHE = {}


def run(inputs, L, T, ncores=8):
    lay = host_layout(inputs, L)
    key = (L, T)
    if key not in _CACHE:
        _CACHE[key] = build_program(L, T)
    nc = _CACHE[key]
    x = np.asarray(inputs["x"], np.float32)
    c = np.asarray(inputs["c"], np.float32)
    in_maps = []
    for b in range(ncores):
        m = dict(lay)
        m["xT"] = np.ascontiguousarray(x[b, :T].T)
        m["cT"] = np.ascontiguousarray(c[b].reshape(8, 128).T)
        in_maps.append(m)
    res = run_bass_kernel_spmd(nc, in_maps, core_ids=list(range(ncores)))
    return np.stack([np.ascontiguousarray(r["outT"].T) for r in res.results], axis=0)


def kernel(**inputs):
    inputs = {k: np.asarray(v) for k, v in inputs.items()}
    return run(inputs, 4, 4096).astype(np.float32)
```

```python
import numpy as np
from contextlib import ExitStack
import concourse.bass as bass
import concourse.mybir as mybir
from concourse.bass_utils import run_bass_kernel_spmd

F32 = mybir.dt.float32
BF16 = mybir.dt.bfloat16
ALU = mybir.AluOpType
AF = mybir.ActivationFunctionType

D = 1024
DFF = 2816
NJ = 22
NT = 1024
NS = 512
CH = 128
NC = NS // CH
NLEV = 6
NV = 160
C0 = 0.6065306597126334
EPS = 1e-6
GN_EPS = 64e-5

import os
PARTS = set(os.environ.get("MK_PARTS", "ffn1,att,rwkv,conv,ffn2").split(","))
ENGS = ("pe", "act", "dve", "pool", "sp")
SEM_ROT = 12000
SAME_ENGINE_SYNC = True


class Op:
    __slots__ = ("eng", "fn", "deps", "idx", "marked", "cnt", "is_dma", "slot", "dval", "semi")

    def __init__(self, eng, fn):
        self.eng = eng
        self.fn = fn
        self.deps = set()
        self.marked = False
        self.cnt = None
        self.is_dma = False
        self.slot = None
        self.dval = None
        self.semi = 0


class Prog:
    def __init__(self, nc):
        self.nc = nc
        self.ops = {e: [] for e in ENGS}
        self.last_w = {}
        self.readers = {}
        self.slot_cnt = {}
        self.out_dmas = []
        self.bar = {}

    def _add(self, o, reads, writes):
        deps = o.deps
        for k in reads:
            w = self.last_w.get(k)
            if w is not None:
                deps.add(w)
        for k in writes:
            w = self.last_w.get(k)
            if w is not None:
                deps.add(w)
            for r in self.readers.get(k, ()):
                deps.add(r)
        b = self.bar.pop(o.eng, None)
        if b:
            deps.update(b)
        deps.discard(o)
        for k in reads:
            self.readers.setdefault(k, []).append(o)
        for k in writes:
            self.last_w[k] = o
            self.readers[k] = []
        o.idx = len(self.ops[o.eng])
        self.ops[o.eng].append(o)
        return o

    def op(self, eng, fn, reads=(), writes=()):
        return self._add(Op(eng, fn), reads, writes)

    def dma(self, eng, fn, slot, reads=(), writes=(), is_out=False):
        o = Op(eng, fn)
        o.is_dma = True
        o.slot = slot
        self.slot_cnt[slot] = self.slot_cnt.get(slot, 0) + 16
        o.dval = self.slot_cnt[slot]
        self._add(o, reads, writes)
        if is_out:
            self.out_dmas.append(o)
        return o

    def mark(self, name):
        if not hasattr(self, "marks"):
            self.marks = []
        self.marks.append((name, len(self.ops["pe"])))

    def barrier(self):
        last = []
        for e in ("pe", "act", "dve"):
            for o in reversed(self.ops[e]):
                if not o.is_dma:
                    last.append(o)
                    break
        for e in ENGS:
            if e != "pool":
                self.bar[e] = list(last)

    def emit(self):
        nc = self.nc
        for e in ENGS:
            for o in self.ops[e]:
                for d in o.deps:
                    if not d.is_dma:
                        if d.eng == o.eng and (d.eng == "pe" or not SAME_ENGINE_SYNC):
                            continue
                        d.marked = True
        nsem = {}
        for e in ENGS:
            c = 0
            semi = 0
            for o in self.ops[e]:
                if o.is_dma:
                    continue
                if o.marked:
                    if c >= SEM_ROT:
                        semi += 1
                        c = 0
                    c += 1
                    o.cnt = c
                    o.semi = semi
            nsem[e] = semi + 1
        slots = sorted(self.slot_cnt.keys(), key=str)
        with ExitStack() as es:
            esem = {e: [es.enter_context(nc.semaphore(f"s_{e}_{i}")) for i in range(nsem[e])] for e in ENGS}
            dsem = {s: es.enter_context(nc.semaphore(f"d_{i}")) for i, s in enumerate(slots)}
            block = es.enter_context(nc.Block())
            engmap = {"pe": block.tensor, "act": block.scalar, "dve": block.vector,
                      "pool": block.gpsimd, "sp": block.sync}

            def make(e):
                def body(eng):
                    waited = {}
                    for o in self.ops[e]:
                        for d in sorted(o.deps, key=lambda d: (d.eng, d.idx)):
                            if d.is_dma:
                                key = ("d", d.slot)
                                val = d.dval
                                sem = dsem[d.slot]
                            else:
                                if d.eng == e and (e == "pe" or not SAME_ENGINE_SYNC):
                                    continue
                                key = ("e", d.eng, d.semi)
                                val = d.cnt
                                sem = esem[d.eng][d.semi]
                            if waited.get(key, 0) >= val:
                                continue
                            waited[key] = val
                            eng.wait_ge(sem, val)
                        ins = o.fn(eng)
                        if o.is_dma:
                            ins.then_inc(dsem[o.slot], 16)
                        elif o.marked:
                            ins.then_inc(esem[e][o.semi], 1)
                    if e == "sp":
                        for o in self.out_dmas:
                            eng.wait_ge(dsem[o.slot], self.slot_cnt[o.slot])
                return body

            for e in ENGS:
                engmap[e](make(e))
        return nc


MIX_BLOCKS = None


def _mix_cols():
    blocks = []
    for i in range(4):
        blocks.append(np.arange(i * 128, (i + 1) * 128))
    blocks.append(np.arange(512, 640))
    blocks.append(np.concatenate([np.arange(576, 640), np.arange(512, 576)]))
    blocks.append(np.concatenate([np.arange(640, 704), np.arange(640, 704)]))
    blocks.append(np.concatenate([np.arange(704, 768), np.arange(704, 768)]))
    for c in range(6, 20):
        blocks.append(np.arange(c * 128, (c + 1) * 128))
    return np.concatenate(blocks)


def host_layout(inp, L):
    f = lambda a: np.ascontiguousarray(a, dtype=np.float32)
    out = {}

    def ffn_in(w):
        w = w.reshape(L, 8, 128, 2, NJ, 128)
        return f(w.transpose(0, 4, 2, 3, 1, 5).reshape(L, NJ, 128, 2048))

    def ffn_out(w):
        w = w.reshape(L, NJ, 128, 8, 128)
        return f(w.transpose(0, 3, 2, 1, 4).reshape(L, 8, 128, NJ * 128))

    out["w1i"] = ffn_in(inp["w_ffn1_in"][:L])
    out["w1o"] = ffn_out(inp["w_ffn1_out"][:L])
    out["w2i"] = ffn_in(inp["w_ffn2_in"][:L])
    out["w2o"] = ffn_out(inp["w_ffn2_out"][:L])
    wm = inp["w_mix_in"][:L][:, :, _mix_cols()]
    wm = wm.reshape(L, 8, 128, 11, 2, 128)
    out["wmi"] = f(wm.transpose(0, 3, 2, 4, 1, 5).reshape(L, 11, 128, 2048))
    wo = inp["w_mix_out"][:L]
    woz = np.zeros((L, 8, 128, 10, 128), np.float32)
    for s in range(4):
        woz[:, :, :, s, :] = wo[:, s * 128:(s + 1) * 128, :].reshape(L, 128, 8, 128).transpose(0, 2, 1, 3)
    for s in range(2):
        woz[:, :, :, 4 + s, :] = wo[:, 768 + s * 128:768 + (s + 1) * 128, :].reshape(L, 128, 8, 128).transpose(0, 2, 1, 3)
    for hd in range(4):
        woz[:, :, 0:64, 6 + hd, :] = wo[:, 512 + hd * 64:512 + (hd + 1) * 64, :].reshape(L, 64, 8, 128).transpose(0, 2, 1, 3)
    out["wmo"] = f(woz.reshape(L, 8, 128, 1280))
    wsm = np.zeros((L, 128, 3, 256), np.float32)
    wsm[:, 0:64, 0, :] = inp["rwkv_w_w2"][:L]
    wsm[:, 0:64, 1, :] = inp["rwkv_a_w2"][:L]
    wsm[:, :, 2, :] = inp["rwkv_g_w2"][:L]
    out["wsm"] = f(wsm)
    wa = inp["w_ada"][:L].reshape(L, 8, 128, 18, 512)
    out["wada"] = f(wa.transpose(0, 3, 2, 1, 4).reshape(L, 18, 128, 4096))
    vec = np.zeros((L, 128, NV), np.float32)
    fm = lambda v: v.reshape(L, -1, 128).transpose(0, 2, 1)
    vec[:, :, 0:8] = fm(inp["g_ffn1"][:L])
    vec[:, :, 8:16] = fm(inp["g_mix"][:L])
    vec[:, :, 16:24] = fm(inp["g_ffn2"][:L])
    vec[:, :, 24:96] = fm(inp["b_ada"][:L])
    vec[:, :, 96] = np.tile(inp["att_q_gain"][:L], (1, 2))
    vec[:, :, 97] = np.tile(inp["att_k_gain"][:L], (1, 2))
    vec[:, :, 98:106] = inp["att_sinks"][:L][:, None, :]
    mu = inp["rwkv_mu"][:L]
    hs = lambda v: v.reshape(L, 4, 64).transpose(0, 2, 1)
    for q in range(3):
        vec[:, 0:64, 106 + q * 4:110 + q * 4] = hs(mu[:, q * 256:(q + 1) * 256])
    vec[:, 0:64, 118] = mu[:, 768:832]
    vec[:, 0:64, 119] = mu[:, 832:896]
    vec[:, :, 120] = mu[:, 896:1024]
    vec[:, 0:64, 121:125] = hs(inp["rwkv_w0"][:L])
    vec[:, 0:64, 125:129] = hs(inp["rwkv_a0"][:L])
    vec[:, 0:64, 129:133] = hs(inp["rwkv_k_k"][:L])
    vec[:, 0:64, 133:137] = hs(inp["rwkv_k_a"][:L])
    vec[:, 0:64, 137:141] = hs(inp["rwkv_r_k"][:L].reshape(L, 256))
    vec[:, 0:64, 141:145] = hs(inp["rwkv_gn_w"][:L])
    vec[:, 0:64, 145:149] = hs(inp["rwkv_gn_b"][:L])
    cw = inp["conv_w"][:L]
    for c in range(2):
        for j in range(3):
            vec[:, :, 149 + c * 3 + j] = cw[:, j, c * 128:(c + 1) * 128]
    vec[:, 0:64, 155] = inp["att_q_gain"][:L]
    vec[:, 64:128, 156] = inp["att_q_gain"][:L]
    out["vec"] = f(vec.transpose(1, 0, 2))
    cst = np.zeros((128, 8, 128), np.float32)
    cst[:, 0] = np.eye(128)
    cst[:, 1] = 1.0
    cst[0:64, 2, 0:64] = 1.0
    cst[64:128, 2, 64:128] = 1.0
    cst[0:64, 3, 0:64] = 1.0 / 64.0
    jj = np.arange(128)[:, None]
    ii = np.arange(128)[None, :]
    cst[:, 4] = (jj <= ii)
    cst[:, 5] = (jj > ii)
    j6 = np.arange(64)[:, None]
    t6 = np.arange(64)[None, :]
    cst[:, 6] = (ii > jj)
    out["cst"] = cst
    rm = np.ones((64, NS), np.float32)
    rm[:, ::CH] = 0.0
    out["rmask"] = rm
    return out


def build_program(L, T, debug=False):
    nc = bass.Bass("TRN2", target_bir_lowering=False)
    NP = T // NT
    dr = {}

    def din(name, shape):
        dr[name] = nc.dram_tensor(name, list(shape), F32, kind="ExternalInput").ap()
        return dr[name]

    xT = din("xT", (D, T))
    cT = din("cT", (128, 8))
    w1i = din("w1i", (L, NJ, 128, 2048))
    w1o = din("w1o", (L, 8, 128, NJ * 128))
    w2i = din("w2i", (L, NJ, 128, 2048))
    w2o = din("w2o", (L, 8, 128, NJ * 128))
    wmi = din("wmi", (L, 11, 128, 2048))
    wmo = din("wmo", (L, 8, 128, 1280))
    wsmd = din("wsm", (L, 128, 3, 256))
    wada = din("wada", (L, 18, 128, 4096))
    vecd = din("vec", (128, L, NV))
    cstd = din("cst", (128, 8, 128))
    rmd = din("rmask", (64, NS))
    outT = nc.dram_tensor("outT", [D, T], F32, kind="ExternalOutput").ap()

    P = Prog(nc)
    es = ExitStack()
    with es:
        def sb(name, shape, dt=F32):
            return es.enter_context(nc.sbuf_tensor("sb_" + name, list(shape), dt))

        h = sb("h", (128, 8, NT))
        hn = sb("hn", (128, 8, NT), BF16)
        NWS = 3
        wslot = [sb(f"wslot{i}", (128, 2816), BF16) for i in range(NWS)]
        vec = sb("vec", (128, L, NV))
        mods = sb("mods", (128, L, 72))
        coef = sb("coef", (128, L, 48))
        omka = sb("omka", (64, L, 4))
        esink = sb("esink", (128, L, 8))
        cact = sb("cact", (128, 8))
        cstb = sb("cstb", (128, 8, 128), BF16)
        rmask = sb("rmask", (64, NS), BF16)
        identrep = sb("identrep", (CH, NC, CH), BF16)
        mrep = sb("mrep", (CH, 2, 2 * CH), BF16)
        mrepL = sb("mrepL", (CH, NC, CH), BF16)
        amask = sb("amask", (128, 2, 512), BF16)
        wsm1 = sb("wsm", (128, 3, 256), BF16)
        kcar = sb("kcar", (128, L, 2, 128), BF16)
        vcar = sb("vcar", (128, L, 256), BF16)
        rwlast = sb("rwlast", (128, L, 15))
        ucar = sb("ucar", (128, L, 2, 2))
        Sst = sb("Sst", (64, L, 4, 64))
        hid = sb("hid", (128, NJ, NT), BF16)
        hidflat = hid[:, :, :].rearrange("p a b -> p (a b)")
        praw = hidflat[:, 0:15390].bitcast(F32).rearrange("p (a b) -> p a b", b=NS + 1)
        Xl = [hidflat[0:64, 15392 + i * 1024:15392 + (i + 1) * 1024].bitcast(F32) for i in range(6)]
        cstf = hidflat[:, 16384:18432].bitcast(F32).rearrange("p (a b) -> p a b", b=128)
        regA = sb("regA", (128, 8704), BF16)
        qn = regA[:, 0:4096].rearrange("p (a b) -> p a b", b=NS)
        kbuf = regA[:, 4096:5376].rearrange("p (a b) -> p a b", b=NS + 128)
        vbuf = regA[:, 5376:6656].rearrange("p (a b) -> p a b", b=256)
        PT = regA[:, 6656:8704].rearrange("p (a b c) -> p a b c", a=2, b=2)
        r8 = lambda lo, n: regA[:, lo:lo + NC * n].rearrange("p (a b) -> p a b", b=n)
        E1 = r8(0, 2 * CH)
        E2 = r8(1024, 2 * CH)
        E3 = r8(2048, CH)
        An = [r8(2560, CH), r8(3072, CH)]
        AnT = [r8(3584, CH), r8(4096, CH)]
        Tn = [r8(4608, CH), r8(5120, CH)]
        Zm = r8(5632, 64)
        regB = sb("regB", (128, 2048))
        rb8 = lambda lo, n: regB[0:64, lo:lo + NC * n].rearrange("p (a b) -> p a b", b=n)
        Rp, Pc, Ov, Qc = rb8(0, CH), rb8(512, 64), rb8(768, CH), rb8(1280, 64)
        ubuf = regB[:, 0:2 * (NS + 2)].rearrange("p (a b) -> p a b", b=NS + 2)
        ymix = sb("ymix", (128, 6, NS), BF16)
        yr = sb("yr", (64, 4, NS), BF16)
        t512 = [sb(f"t512_{i}", (128, NS)) for i in range(4)]
        b512 = [sb(f"b512_{i}", (128, NS), BF16) for i in range(3)]
        Xh = [sb(f"X{i}", (64, NS)) for i in range(6, 9)]
        X = Xl + [x[:, :] for x in Xh]
        bons = [X[8], t512[3][0:64, :]]
        ARs = [sb(f"AR{i}", (64, NC, 2, CH), BF16) for i in range(2)]
        BK = sb("BK", (64, NC, 2, CH), BF16)
        Bh = sb("Bh", (64, NS), BF16)
        Kh = sb("Kh", (64, NS), BF16)
        Vb = sb("Vb", (64, NS), BF16)
        tok = sb("tok", (CH, NC, 4, 64), BF16)
        Apt = sb("Apt", (CH, NC, 64), BF16)
        Uvt = sb("Uvt", (CH, NC, 64), BF16)
        gCs = [sb(f"gC{i}", (64, NC)) for i in range(2)]
        sgb = sb("sgb", (128, NS), BF16)
        twd = sb("twd", (64, NS), BF16)
        adb = sb("adb", (64, NS), BF16)
        ps = [es.enter_context(nc.psum_tensor(f"ps{i}", [128, 512], F32)) for i in range(8)]

        P.pe_w = []

        def mm(out, lhsT, rhs, start, stop, reads, writes):
            P.pe_w.append(2 if lhsT.dtype == F32 else 1)
            P.op("pe", lambda e: e.matmul(out, lhsT=lhsT, rhs=rhs, start=start, stop=stop), reads, writes)

        def act(out, in_, func, reads, writes, scale=None, bias=None):
            kw = {}
            if scale is not None:
                kw["scale"] = scale
            if bias is not None:
                kw["bias"] = bias
            P.op("act", lambda e: e.activation(out=out, in_=in_, func=func, **kw), reads, writes)

        def tt(out, in0, in1, op, reads, writes, eng="dve"):
            P.op(eng, lambda e: e.tensor_tensor(out=out, in0=in0, in1=in1, op=op), reads, writes)

        def ts(out, in0, s1, op0, reads, writes, s2=None, op1=None, eng="dve"):
            if op1 is None:
                P.op(eng, lambda e: e.tensor_scalar(out=out, in0=in0, scalar1=s1, scalar2=None, op0=op0), reads, writes)
            else:
                P.op(eng, lambda e: e.tensor_scalar(out=out, in0=in0, scalar1=s1, scalar2=s2, op0=op0, op1=op1), reads, writes)

        def stt(out, in0, scalar, in1, op0, op1, reads, writes):
            P.op("dve", lambda e: e.scalar_tensor_tensor(out=out, in0=in0, scalar=scalar, in1=in1, op0=op0, op1=op1), reads, writes)

        def rsqrt(out, in_, bias, reads, writes):
            act(out, in_, AF.Ln, reads, writes, bias=float(bias))
            act(out, out, AF.Exp, writes, writes, scale=-0.5)

        def cp(out, in_, reads, writes, eng="act"):
            if eng == "act":
                P.op("act", lambda e: e.copy(out=out, in_=in_), reads, writes)
            else:
                P.op(eng, lambda e: e.tensor_copy(out=out, in_=in_), reads, writes)

        wctr = [0]

        def wload(src, ncols):
            i = wctr[0] % NWS
            wctr[0] += 1
            dst = wslot[i][:, 0:ncols]
            P.dma("pool", lambda e: e.dma_start(out=dst, in_=src), ("w", i), writes=[("w", i)])
            return wslot[i], ("w", i)

        psc = [0]

        def nps():
            i = psc[0] % 8
            psc[0] += 1
            return ps[i], ("ps", i)

        P.dma("sp", lambda e: e.dma_start(out=vec[:], in_=vecd[:, :, :]), "vec", writes=["vec"])
        P.dma("sp", lambda e: e.dma_start(out=cstf[:], in_=cstd[:, :, :]), "cst", writes=["cstf"])
        P.dma("pool", lambda e: e.dma_start(out=rmask[:], in_=rmd[:, :]), "rmask", writes=["rmask"])
        P.dma("sp", lambda e: e.dma_start(out=cact[:], in_=cT[:, :]), "cact", writes=["cact0"])
        cp(cstb[:], cstf[:], ["cstf"], ["cstb"], eng="dve")
        ident = cstb[:, 0, :]
        ones = cstb[:, 1, :]
        bdones = cstb[:, 2, :]
        ones64s = cstb[0:64, 3, 0:64]
        ident64 = cstb[0:64, 0, 0:64]
        ident64f = cstf[0:64, 0, 0:64]
        for c4 in range(NC):
            cp(identrep[:, c4, :], cstf[:, 0, :], ["cstf"], ["identrep"], eng="dve")
            cp(mrepL[:, c4, :], cstf[:, 5, :], ["cstf"], ["mrep"], eng="dve")
        for c2 in range(2):
            cp(mrep[:, c2, 0:CH], cstf[:, 6, :], ["cstf"], ["mrep"], eng="dve")
            cp(mrep[:, c2, CH:2 * CH], cstf[:, 4, :], ["cstf"], ["mrep"], eng="dve")
        for c4 in range(4):
            cp(amask[:, 0, c4 * 128:(c4 + 1) * 128], cstf[:, 5, :], ["cstf"], ["amask"], eng="dve")
            cp(amask[:, 1, c4 * 128:(c4 + 1) * 128], cstf[:, 4, :], ["cstf"], ["amask"], eng="dve")
        act(cact[:], cact[:], AF.Silu, ["cact0"], ["cact"])
        for t_, nm in ((kcar, "kcar"), (vcar, "vcar"), (rwlast, "rwlast"), (ucar, "ucar"), (Sst, "S")):
            P.op("dve", lambda e, t_=t_: e.memset(t_[:], 0.0), (), [nm])
        adaf = [hidflat[:, 0:4096], hidflat[:, 4096:8192]]
        cactb = sb("cactb", (128, 8), BF16)
        cp(cactb[:], cact[:], ["cact"], ["cactb"], eng="dve")
        for l in range(L):
            pm, pk = nps()
            for blk in range(18):
                st = adaf[blk % 2]
                key = ("ada", blk % 2)
                P.dma("pool", lambda e, st=st, l=l, blk=blk: e.dma_start(out=st, in_=wada[l, blk]), key, writes=[key])
                stv = st.rearrange("p (k n) -> p k n", k=8)
                for oc in range(4):
                    col = blk * 4 + oc
                    for kc in range(8):
                        mm(pm[:, col:col + 1], stv[:, kc, oc * 128:(oc + 1) * 128], cactb[:, kc:kc + 1],
                           kc == 0, kc == 7, [key, "cactb"], [pk])
            tt(mods[:, l, :], pm[:, 0:72], vec[:, l, 24:96], ALU.add, [pk, "vec"], ["mods"])
            for i, (msc, mgt, gcol, half) in enumerate(((1, 2, 0, 0.5), (4, 5, 8, 1.0), (7, 8, 16, 0.5))):
                ts(coef[:, l, i * 16:i * 16 + 8], mods[:, l, msc * 8:msc * 8 + 8], 1.0, ALU.add, ["mods"], ["coef"], s2=32.0, op1=ALU.mult)
                tt(coef[:, l, i * 16:i * 16 + 8], coef[:, l, i * 16:i * 16 + 8], vec[:, l, gcol:gcol + 8], ALU.mult, ["coef", "vec"], ["coef"])
                ts(coef[:, l, i * 16 + 8:i * 16 + 16], mods[:, l, mgt * 8:mgt * 8 + 8], 1.0, ALU.add, ["mods"], ["coef"], s2=half, op1=ALU.mult)
            ts(omka[:, l, :], vec[0:64, l, 133:137], -1.0, ALU.mult, ["vec"], ["omka"], s2=1.0, op1=ALU.add)
            act(esink[:, l, :], vec[:, l, 98:106], AF.Exp, ["vec"], ["esink"])
        P.barrier()

        def rmsnorm(l, which, t0, n):
            acol = which * 16
            bcol = (0, 3, 6)[which] * 8
            for s in range(n // NS):
                c0 = t0 + s * NS
                pss, psk = nps()
                for kc in range(8):
                    sq = b512[kc % 2]
                    act(sq[:], h[:, kc, c0:c0 + NS], AF.Square, [("h", kc)], [("b512", kc % 2)])
                    mm(pss[:], ones, sq[:], kc == 0, kc == 7, [("b512", kc % 2), "cstb"], [psk])
                rs = t512[0]
                rsqrt(rs[:], pss[:], D * EPS, [psk], [("t512", 0)])
                for kc in range(8):
                    tmp = t512[1 + kc % 2]
                    tt(tmp[:], h[:, kc, c0:c0 + NS], rs[:], ALU.mult, [("h", kc), ("t512", 0)], [("t512", 1 + kc % 2)])
                    act(hn[:, kc, c0:c0 + NS], tmp[:], AF.Identity, [("t512", 1 + kc % 2), "coef", "mods"], [("hn", kc)],
                        scale=coef[:, l, acol + kc:acol + kc + 1], bias=mods[:, l, bcol + kc:bcol + kc + 1])

        def ffn(l, which, wi, wo):
            rmsnorm(l, which, 0, NT)
            ccol = which * 16 + 8
            for j in range(NJ):
                w, wk = wload(wi[l, j], 2048)
                wv = w[:, 0:2048].rearrange("p (t k n) -> p t k n", t=2, k=8)
                for s in range(2):
                    pg, pgk = nps()
                    pu, puk = nps()
                    for kc in range(8):
                        mm(pg[:], wv[:, 0, kc, :], hn[:, kc, s * NS:(s + 1) * NS], kc == 0, kc == 7, [wk, ("hn", kc)], [pgk])
                    for kc in range(8):
                        mm(pu[:], wv[:, 1, kc, :], hn[:, kc, s * NS:(s + 1) * NS], kc == 0, kc == 7, [wk, ("hn", kc)], [puk])
                    sg = t512[(j * 2 + s) % 2 + 1]
                    sgk = ("t512", (j * 2 + s) % 2 + 1)
                    act(sg[:], pg[:], AF.Silu, [pgk], [sgk])
                    tt(hid[:, j, s * NS:(s + 1) * NS], sg[:], pu[:], ALU.mult, [sgk, puk], [("hid", j, s)])
            for m in range(8):
                w, wk = wload(wo[l, m], NJ * 128)
                wv = w[:, 0:NJ * 128].rearrange("p (k n) -> p k n", k=NJ)
                for s in range(2):
                    po, pok = nps()
                    for j in range(NJ):
                        mm(po[:], wv[:, j, :], hid[:, j, s * NS:(s + 1) * NS], j == 0, j == NJ - 1, [wk, ("hid", j, s)], [pok])
                    stt(h[:, m, s * NS:(s + 1) * NS], po[:], coef[:, l, ccol + m:ccol + m + 1], h[:, m, s * NS:(s + 1) * NS],
                        ALU.mult, ALU.add, [pok, "coef", ("h", m)], [("h", m)])

        def mixer(l, st, first):
            t0 = st * NS
            P.barrier()
            if st == 0:
                P.dma("pool", lambda e: e.dma_start(out=wsm1[:], in_=wsmd[l]), "wsm", writes=["wsm"])
            PRK = [("praw", i) for i in range(15)]
            P.mark("mix_start")
            rmsnorm(l, 1, t0, NS)
            P.mark("mix_norm_done")
            hs = lambda kc: hn[:, kc, t0:t0 + NS]
            HN = [("hn", kc) for kc in range(8)]
            cp(kbuf[:, :, 0:128], kcar[:, l], ["kcar"], ["kbuf"])
            cp(vbuf[:, 0, :], vcar[:, l], ["vcar"], ["vbuf"])
            cp(praw[:, :, 0:1], rwlast[:, l, :].rearrange("p (a b) -> p a b", b=1), ["rwlast"], PRK)

            def qknorm(pp, ppk, gcol, dst, dkey, dst2=None):
                sq = b512[2]
                act(sq[:], pp[:], AF.Square, [ppk], [("b512", 2)])
                p2, p2k = nps()
                mm(p2[:], bdones, sq[:], True, True, [("b512", 2), "cstb"], [p2k])
                rs = t512[3]
                rsqrt(rs[:], p2[:], 64 * EPS, [p2k], [("t512", 3)])
                if dst2 is None:
                    stt(dst, pp[:], vec[:, l, gcol:gcol + 1], rs[:], ALU.mult, ALU.mult, [ppk, "vec", ("t512", 3)], [dkey])
                else:
                    stt(dst, pp[:], vec[:, l, 155:156], rs[:], ALU.mult, ALU.mult, [ppk, "vec", ("t512", 3)], [dkey])
                    stt(dst2, pp[:], vec[:, l, 156:157], rs[:], ALU.mult, ALU.mult, [ppk, "vec", ("t512", 3)], [dkey])

            def getw(i):
                w, wk = wload(wmi[l, i], 2048)
                v = w[:, 0:2048].rearrange("p (t k n) -> p t k n", t=2, k=8)
                return v, wk

            def pj(v, wk, t, m0, m1):
                pp, ppk = nps()
                npart = m1 - m0
                for kc in range(8):
                    mm(pp[0:npart, :], v[:, t, kc, m0:m1], hs(kc), kc == 0, kc == 7, [wk, HN[kc]], [ppk])
                return pp, ppk

            if "att" not in PARTS or "rwkv" not in PARTS or "conv" not in PARTS:
                P.op("dve", lambda e: e.memset(ymix[:], 0.0), (), ["ymix"])
                P.op("dve", lambda e: e.memset(yr[:], 0.0), (), [("yr", i) for i in range(4)])
            def att_gen():
                for i in range(2 if "att" in PARTS else 0):
                    v, wk = getw(i)
                    for t in range(2):
                        pp, ppk = pj(v, wk, t, 0, 128)
                        qknorm(pp, ppk, 96, qn[:, (i * 2 + t) * 2, :], "qn", dst2=qn[:, (i * 2 + t) * 2 + 1, :])
                        yield
                v, wk = getw(2)
                for t in range(2 if "att" in PARTS else 0):
                    pp, ppk = pj(v, wk, t, 0, 128)
                    qknorm(pp, ppk, 97, kbuf[:, t, 128:128 + NS], "kbuf")
                    yield
                v, wk = getw(3)
                for b in range(4 if "att" in PARTS else 0):
                    pp, ppk = nps()
                    for kc in range(8):
                        mm(pp[:, 0:128], hn[:, kc, t0 + b * 128:t0 + (b + 1) * 128], v[:, 0, kc, :], kc == 0, kc == 7, [wk, HN[kc]], [ppk])
                    for kc in range(8):
                        mm(pp[:, 128:256], hn[:, kc, t0 + b * 128:t0 + (b + 1) * 128], v[:, 1, kc, :], kc == 0, kc == 7, [wk, HN[kc]], [ppk])
                    cp(vbuf[:, 1 + b, :], pp[:, 0:256], [ppk], ["vbuf"])
                    yield
                pass
                for b in range(4 if "att" in PARTS else 0):
                    for g in range(2):
                        kbs = (1,) if (first and b == 0) else (0, 1)
                        pv, pvk = nps()
                        pd, pdk = nps()
                        pb = (b * 2 + g) % 2
                        for kb in kbs:
                            sp_, spk = nps()
                            for j in range(4):
                                hd = 4 * g + j
                                half = hd % 2
                                kt = 0 if g == half else 1
                                kcol = (b + kb) * 128
                                mm(sp_[:, j * 128:(j + 1) * 128],
                                   kbuf[:, kt, kcol:kcol + 128],
                                   qn[:, hd, b * 128:(b + 1) * 128],
                                   True, True, ["kbuf", "qn"], [spk])
                            act(PT[:, pb, kb, :], sp_[:], AF.Exp, [spk], [("PT", pb, kb)], scale=8.0)
                            tt(PT[:, pb, kb, :], PT[:, pb, kb, :], amask[:, kb, :], ALU.mult, [("PT", pb, kb), "amask"], [("PT", pb, kb)])
                        yield
                        for n_, kb in enumerate(kbs):
                            mm(pv[:], vbuf[:, b + kb, g * 128:(g + 1) * 128], PT[:, pb, kb, :], n_ == 0, n_ == len(kbs) - 1, ["vbuf", ("PT", pb, kb)], [pvk])
                        for n_, kb in enumerate(kbs):
                            mm(pd[:], ones, PT[:, pb, kb, :], n_ == 0, n_ == len(kbs) - 1, ["cstb", ("PT", pb, kb)], [pdk])
                        den = t512[3]
                        for j in range(4):
                            act(den[:, j * 128:(j + 1) * 128], pd[:, j * 128:(j + 1) * 128], AF.Ln, [pdk, "esink"], [("t512", 3)],
                                bias=esink[:, l, 4 * g + j:4 * g + j + 1])
                        act(den[:], den[:], AF.Exp, [("t512", 3)], [("t512", 3)], scale=-1.0)
                        for j in range(4):
                            hd = 4 * g + j
                            half = hd % 2
                            sl = slice(half * 64, (half + 1) * 64)
                            tt(ymix[sl, hd // 2, b * 128:(b + 1) * 128], pv[sl, j * 128:(j + 1) * 128], den[sl, j * 128:(j + 1) * 128],
                               ALU.mult, [pvk, ("t512", 3)], ["ymix"])
                        yield

                cp(kcar[:, l], kbuf[:, :, NS:NS + 128], ["kbuf"], ["kcar"])
                cp(vcar[:, l], vbuf[:, 4, :], ["vbuf"], ["vcar"])
                yield

            def rwp_gen():
                pass
                for i in range(4, 8 if "rwkv" in PARTS else 4):
                    v, wk = getw(i)
                    for t in range(2):
                        cidx = 6 + (i - 4) * 2 + t
                        if cidx == 13:
                            pp, ppk = pj(v, wk, t, 0, 128)
                            cp(praw[:, 14, 1:NS + 1], pp[:], [ppk], [("praw", 14)])
                            yield
                        else:
                            for hf in range(2):
                                slot = (cidx - 6) * 2 + hf
                                pp, ppk = pj(v, wk, t, hf * 64, (hf + 1) * 64)
                                cp(praw[0:64, slot, 1:NS + 1], pp[0:64, :], [ppk], [("praw", slot)])
                                yield
                cp(rwlast[:, l, :].rearrange("p (a b) -> p a b", b=1), praw[:, :, NS:NS + 1], PRK, ["rwlast"])
                for slot in range(15 if "rwkv" in PARTS else 0):
                    npt = 128 if slot == 14 else 64
                    mcol = 106 + slot if slot < 12 else (118 + slot - 12)
                    d = t512[slot % 2]
                    dk = ("t512", slot % 2)
                    tt(d[0:npt, :], praw[0:npt, slot, 0:NS], praw[0:npt, slot, 1:NS + 1], ALU.subtract, [("praw", slot)], [dk])
                    stt(praw[0:npt, slot, 1:NS + 1], d[0:npt, :], vec[0:npt, l, mcol:mcol + 1], praw[0:npt, slot, 1:NS + 1],
                        ALU.mult, ALU.add, [dk, "vec", ("praw", slot)], [("praw", slot)])
                    if slot % 2 == 1:
                        yield
                act(twd[:], praw[0:64, 12, 1:NS + 1], AF.Tanh, [("praw", 12)], ["twd"])
                cp(adb[:], praw[0:64, 13, 1:NS + 1], [("praw", 13)], ["adb"])
                act(sgb[:], praw[:, 14, 1:NS + 1], AF.Sigmoid, [("praw", 14)], ["sgb"])


                yield

            gens = [att_gen(), rwp_gen()]
            while gens:
                for g_ in list(gens):
                    try:
                        next(g_)
                    except StopIteration:
                        gens.remove(g_)
            P.barrier()
            P.mark("att_done")
            P.mark("rwproj_done")
            c3 = lambda ap: ap.rearrange("p (c t) -> p c t", t=CH)
            def head_prep(hd):
                r_ = praw[0:64, hd, 1:NS + 1]
                k_ = praw[0:64, 4 + hd, 1:NS + 1]
                v_ = praw[0:64, 8 + hd, 1:NS + 1]
                hcol = lambda base: vec[0:64, l, base + hd:base + hd + 1]
                rK, kK, vK = ("praw", hd), ("praw", 4 + hd), ("praw", 8 + hd)
                xs, xcs, xa, xe1, xe2, xt, xkk, xkm, xbo = X
                XK = [("X", i) for i in range(9)]
                hsl = slice(hd * 64, (hd + 1) * 64)
                par = hd % 2
                AR = ARs[par]
                gC = gCs[par]
                xbo = bons[par]
                A0k, A1k, gCk, bok = ("AR0", par), ("AR1", par), ("gC", par), ("bon", par)
                pz, pzk = nps()
                mm(pz[0:64, :], wsm1[0:64, 0, hsl], twd[:], True, True, ["wsm", "twd"], [pzk])
                act(xs, pz[0:64, :], AF.Sigmoid, [pzk, "vec"], [XK[0]], bias=hcol(121))
                yield
                pa, pak = nps()
                mm(pa[0:64, :], wsm1[0:64, 1, hsl], adb[:], True, True, ["wsm", "adb"], [pak])
                act(xa, pa[0:64, :], AF.Sigmoid, [pak, "vec"], [XK[2]], bias=hcol(125))
                yield
                P.op("dve", lambda e, xcs=xcs, xs=xs: e.tensor_tensor_scan(out=xcs, data0=rmask[:], data1=xs, initial=0.0, op0=ALU.mult, op1=ALU.add),
                     ["rmask", XK[0]], [XK[1]])
                act(xe1, xcs, AF.Exp, [XK[1]], [XK[3]], scale=-C0)
                tt(AR[:, :, 1, :], c3(r_), c3(xe1), ALU.mult, [rK, XK[3]], [A1k])
                cp(gC[:], c3(xe1)[:, :, CH - 1], [XK[3]], [gCk], eng="dve")
                yield
                act(xe2, xcs, AF.Exp, [XK[1]], [XK[4]], scale=C0)
                tt(xt, xcs, xs, ALU.subtract, [XK[1], XK[0]], [XK[5]])
                act(xe1, xt, AF.Exp, [XK[5]], [XK[3]], scale=-C0)
                ts(xkk, k_, hcol(129), ALU.mult, [kK, "vec"], [XK[6]])
                yield
                act(b512[2][0:64, :], xkk, AF.Square, [XK[6]], [("b512", 2)])
                pn, pnk = nps()
                mm(pn[0:64, :], cstb[0:64, 1, 0:64], b512[2][0:64, :], True, True, [("b512", 2), "cstb"], [pnk])
                yield
                rsqrt(xt, pn[0:64, :], 1e-18, [pnk], [XK[5]])
                tt(xkk, xkk, xt, ALU.mult, [XK[6], XK[5]], [XK[6]])
                yield
                stt(AR[:, :, 0, :], c3(xkk), -1.0, c3(xe1), ALU.mult, ALU.mult, [XK[6], XK[3]], [A0k])
                tt(xt, xkk, xa, ALU.mult, [XK[6], XK[2]], [XK[5]])
                tt(BK[:, :, 0, :], c3(xt), c3(xe2), ALU.mult, [XK[5], XK[4]], ["BK0"])
                gCb = gC[:].rearrange("p (c o) -> p c o", o=1).to_broadcast([64, NC, CH])
                tt(c3(Bh[:]), BK[:, :, 0, :], gCb, ALU.mult, ["BK0", gCk], ["Bh"])
                yield
                ts(xkm, xa, hcol(133), ALU.mult, [XK[2], "vec", "omka"], [XK[7]], s2=omka[:, l, hd:hd + 1], op1=ALU.add)
                yield
                tt(xkm, xkm, k_, ALU.mult, [XK[7], kK], [XK[7]])
                tt(BK[:, :, 1, :], c3(xkm), c3(xe2), ALU.mult, [XK[7], XK[4]], ["BK1"])
                tt(c3(Kh[:]), BK[:, :, 1, :], gCb, ALU.mult, ["BK1", gCk], ["Kh"])
                yield
                cp(Vb[:], v_, [vK], ["Vb"])
                stt(b512[2][0:64, :], r_, hcol(137), xkm, ALU.mult, ALU.mult, [rK, "vec", XK[7]], [("b512", 2)])
                yield
                pbn, pbk = nps()
                mm(pbn[0:64, :], cstb[0:64, 1, 0:64], b512[2][0:64, :], True, True, [("b512", 2), "cstb"], [pbk])
                tt(xbo, pbn[0:64, :], v_, ALU.mult, [pbk, vK], [bok])
                yield

                yield

            def head_rest(hd, fill):
                r_ = praw[0:64, hd, 1:NS + 1]
                k_ = praw[0:64, 4 + hd, 1:NS + 1]
                v_ = praw[0:64, 8 + hd, 1:NS + 1]
                hcol = lambda base: vec[0:64, l, base + hd:base + hd + 1]
                rK, kK, vK = ("praw", hd), ("praw", 4 + hd), ("praw", 8 + hd)
                xs, xcs, xa, xe1, xe2, xt, xkk, xkm, xbo = X
                XK = [("X", i) for i in range(9)]
                hsl = slice(hd * 64, (hd + 1) * 64)
                par = hd % 2
                AR = ARs[par]
                gC = gCs[par]
                xbo = bons[par]
                A0k, A1k, gCk, bok = ("AR0", par), ("AR1", par), ("gC", par), ("bon", par)
                P.mark(f"h{hd}_prep")
                for cp2 in range(NC // 2):
                    ptk, ptkk = nps()
                    for cc in range(2):
                        c = cp2 * 2 + cc
                        srcs = (AR[:, c, 0, :], Bh[:, c * CH:(c + 1) * CH], Kh[:, c * CH:(c + 1) * CH], Vb[:, c * CH:(c + 1) * CH])
                        for q, s_ in enumerate(srcs):
                            o_ = ptk[:, (cc * 4 + q) * 64:(cc * 4 + q + 1) * 64]
                            mm(o_, s_, ident64, True, True, [A0k, "Bh", "Kh", "Vb", "cstb"], [ptkk])
                    cp(tok[:, cp2 * 2:cp2 * 2 + 2, :, :], ptk[:, :].rearrange("p (c q n) -> p c q n", c=2, q=4), [ptkk], ["tok"])
                for (E, ek, lh, lk) in ((E1, "E1", 0, "BK0"), (E2, "E2", 1, "BK1")):
                    for hb in range(NC // 2):
                        pe_, pek = nps()
                        for cc in range(2):
                            c = hb * 2 + cc
                            mm(pe_[:, cc * 2 * CH:(cc + 1) * 2 * CH], BK[:, c, lh, :], AR[:, c, :, :].rearrange("p a t -> p (a t)"),
                               True, True, [lk, A0k, A1k], [pek])
                        tt(E[:, hb * 2:(hb + 1) * 2, :], pe_[:, :].rearrange("p (c n) -> p c n", c=2), mrep[:], ALU.mult, [pek, "mrep"], [ek])
                pe_, pek = nps()
                for c in range(NC):
                    mm(pe_[:, c * CH:(c + 1) * CH], AR[:, c, 0, :], BK[:, c, 0, :], True, True, [A0k, "BK0"], [pek])
                tt(E3[:], pe_[:, :].rearrange("p (c n) -> p c n", c=NC), mrepL[:], ALU.mult, [pek, "mrep"], ["E3"])
                pZ, pZk = nps()
                for c in range(NC):
                    mm(pZ[:, c * 64:(c + 1) * 64], E2[:, c, 0:CH], tok[:, c, 3, :], True, True, ["E2", "tok"], [pZk])
                cp(Zm[:], pZ[:, 0:NC * 64].rearrange("p (c n) -> p c n", c=NC), [pZk], ["Zm"])
                P.mark(f"h{hd}_E")
                tt(Tn[0][:], E1[:, :, 0:CH], identrep[:], ALU.add, ["E1", "identrep"], [("Tn", 0)])
                Anat, AT = E1[:, :, 0:CH], E3[:]
                Ak, ATk = "E1", "E3"
                nlev = NLEV
                vc = lambda p_: p_[:, :].rearrange("p (c n) -> p c n", c=NC)
                for n in range(1, nlev + 1):
                    pi = n % 2
                    psq, psqk = nps()
                    for c in range(NC):
                        mm(psq[:, c * CH:(c + 1) * CH], Anat[:, c, :], AT[:, c, :], True, True, [Ak, ATk], [psqk])
                    cp(AnT[pi][:], vc(psq), [psqk], [("AnT", pi)])
                    if n < nlev:
                        psn, psnk = nps()
                        for c in range(NC):
                            mm(psn[:, c * CH:(c + 1) * CH], AT[:, c, :], Anat[:, c, :], True, True, [Ak, ATk], [psnk])
                        cp(An[pi][:], vc(psn), [psnk], [("An", pi)], eng="dve")
                    pt_, ptk_ = nps()
                    for c in range(NC):
                        mm(pt_[:, c * CH:(c + 1) * CH], ident, Tn[1 - pi][:, c, :], True, False, ["cstb", ("Tn", 1 - pi)], [ptk_])
                        mm(pt_[:, c * CH:(c + 1) * CH], AnT[pi][:, c, :], Tn[1 - pi][:, c, :], False, True, [("AnT", pi), ("Tn", 1 - pi)], [ptk_])
                    cp(Tn[pi][:], vc(pt_), [ptk_], [("Tn", pi)], eng="dve")
                    Anat, AT = An[pi][:], AnT[pi][:]
                    Ak, ATk = ("An", pi), ("AnT", pi)
                    fill(3)
                P.mark(f"h{hd}_T")
                Tf = Tn[nlev % 2]
                Tk = ("Tn", nlev % 2)
                pA, pAk = nps()
                pU, pUk = nps()
                for c in range(NC):
                    mm(pA[:, c * 64:(c + 1) * 64], Tf[:, c, :], tok[:, c, 0, :], True, True, [Tk, "tok"], [pAk])
                    mm(pU[:, c * 64:(c + 1) * 64], Tf[:, c, :], Zm[:, c, :], True, True, [Tk, "Zm"], [pUk])
                cp(Apt[:], pA[:, 0:NC * 64].rearrange("p (c n) -> p c n", c=NC), [pAk], ["Apt"], eng="dve")
                cp(Uvt[:], pU[:, 0:NC * 64].rearrange("p (c n) -> p c n", c=NC), [pUk], ["Uvt"])
                fill(3)
                pR, pRk = nps()
                pP, pPk = nps()
                pO, pOk = nps()
                pQ, pQk = nps()
                for c in range(NC):
                    mm(pR[0:64, c * CH:(c + 1) * CH], Apt[:, c, :], E1[:, c, CH:2 * CH], True, True, ["Apt", "E1"], [pRk])
                    mm(pP[0:64, c * 64:(c + 1) * 64], Apt[:, c, :], tok[:, c, 1, :], True, True, ["Apt", "tok"], [pPk])
                for c in range(NC):
                    mm(pO[0:64, c * CH:(c + 1) * CH], Uvt[:, c, :], E1[:, c, CH:2 * CH], True, False, ["Uvt", "E1"], [pOk])
                    mm(pO[0:64, c * CH:(c + 1) * CH], tok[:, c, 3, :], E2[:, c, CH:2 * CH], False, True, ["tok", "E2"], [pOk])
                    mm(pQ[0:64, c * 64:(c + 1) * 64], tok[:, c, 1, :], Uvt[:, c, :], True, False, ["tok", "Uvt"], [pQk])
                    mm(pQ[0:64, c * 64:(c + 1) * 64], tok[:, c, 2, :], tok[:, c, 3, :], False, True, ["tok"], [pQk])
                tt(Rp[:], pR[0:64, :].rearrange("p (c n) -> p c n", c=NC), AR[:, :, 1, :], ALU.add, [pRk, A1k], ["Rp"])
                tt(Pc[:], identrep[0:64, :, 0:64], gC[:].rearrange("p (c o) -> p c o", o=1).to_broadcast([64, NC, 64]), ALU.mult, ["identrep", gCk], ["Pc"])
                tt(Pc[:], Pc[:], pP[0:64, 0:NC * 64].rearrange("p (c n) -> p c n", c=NC), ALU.add, ["Pc", pPk], ["Pc"])
                cp(Ov[:], pO[0:64, :].rearrange("p (c n) -> p c n", c=NC), [pOk], ["Ov"])
                cp(Qc[:], pQ[0:64, 0:NC * 64].rearrange("p (c n) -> p c n", c=NC), [pQk], ["Qc"], eng="dve")
                P.mark(f"h{hd}_pre")
                fill(3)
                Y = t512[2][0:64, :]
                for c in range(NC):
                    Sa = Sst[:, l, hd, :]
                    ka = ("S", l, hd)
                    po_, pok_ = nps()
                    mm(po_[0:64, 0:CH], Sa, Rp[:, c, :], True, True, [ka, "Rp"], [pok_])
                    mm(po_[0:64, CH:CH + 64], Pc[:, c, :], Sa, True, True, [ka, "Pc"], [pok_])
                    tt(Sa, po_[0:64, CH:CH + 64], Qc[:, c, :], ALU.add, [pok_, "Qc"], [ka])
                    tt(Y[:, c * CH:(c + 1) * CH], po_[0:64, 0:CH], Ov[:, c, :], ALU.add, [pok_, "Ov"], [("t512", 2)])
                    fill(2)
                P.mark(f"h{hd}_chain")
                YK = [("t512", 2)]
                xcs, xt = t512[0][0:64, :], t512[1][0:64, :]
                XK = list(XK)
                XK[1], XK[5] = ("t512", 0), ("t512", 1)
                cp(b512[0][0:64, :], Y, YK, [("b512", 0)])
                pm_, pmk = nps()
                mm(pm_[0:64, :], ones64s, b512[0][0:64, :], True, True, [("b512", 0), "cstb"], [pmk])
                tt(xcs, Y, pm_[0:64, :], ALU.subtract, YK + [pmk], [XK[1]])
                act(b512[0][0:64, :], xcs, AF.Square, [XK[1]], [("b512", 0)])
                pv_, pvk_ = nps()
                mm(pv_[0:64, :], ones64s, b512[0][0:64, :], True, True, [("b512", 0), "cstb"], [pvk_])
                rsqrt(xt, pv_[0:64, :], GN_EPS, [pvk_], [XK[5]])
                tt(xcs, xcs, xt, ALU.mult, [XK[1], XK[5]], [XK[1]])
                act(xcs, xcs, AF.Identity, [XK[1], "vec"], [XK[1]], scale=hcol(141), bias=hcol(145))
                tt(xcs, xcs, xbo, ALU.add, [XK[1], bok], [XK[1]])
                pg_, pgk_ = nps()
                mm(pg_[0:64, :], wsm1[:, 2, hsl], sgb[:], True, True, ["wsm", "sgb"], [pgk_])
                tt(yr[:, hd, :], xcs, pg_[0:64, :], ALU.mult, [XK[1], pgk_], [("yr", hd)])


            def advance(gen, n):
                if gen is None:
                    return
                for _ in range(n):
                    try:
                        next(gen)
                    except StopIteration:
                        return

            if "rwkv" in PARTS:
                g0 = head_prep(0)
                advance(g0, 10 ** 6)
                for hd in range(4):
                    gn = head_prep(hd + 1) if hd < 3 else None
                    head_rest(hd, lambda n, gn=gn: advance(gn, n))
                    advance(gn, 10 ** 6)

            P.mark("rwkv_done")
            P.barrier()
            cp(ubuf[:, :, 0:2], ucar[:, l], ["ucar"], ["ubuf"])
            vB, wkB = getw(8)
            Bsb = t512[0]
            vC, wkC = getw(9)
            vH, wkH = getw(10)
            for t in range(2 if "conv" in PARTS else 0):
                pB, pBk = pj(vB, wkB, t, 0, 128)
                pC, pCk = pj(vC, wkC, t, 0, 128)
                pH, pHk = pj(vH, wkH, t, 0, 128)
                cp(t512[0][:], pB[:], [pBk], [("t512", 0)])
                cp(t512[1][:], pC[:], [pCk], [("t512", 1)])
                tt(ubuf[:, t, 2:NS + 2], t512[1][:], pH[:], ALU.mult, [("t512", 1), pHk], ["ubuf"])
                cw = lambda j: vec[:, l, 149 + t * 3 + j:149 + t * 3 + j + 1]
                acc = t512[2]
                ts(acc[:], ubuf[:, t, 2:NS + 2], cw(2), ALU.mult, ["ubuf", "vec"], [("t512", 2)])
                stt(acc[:], ubuf[:, t, 1:NS + 1], cw(1), acc[:], ALU.mult, ALU.add, ["ubuf", "vec", ("t512", 2)], [("t512", 2)])
                stt(acc[:], ubuf[:, t, 0:NS], cw(0), acc[:], ALU.mult, ALU.add, ["ubuf", "vec", ("t512", 2)], [("t512", 2)])
                tt(ymix[:, 4 + t, :], acc[:], t512[0][:], ALU.mult, [("t512", 2), ("t512", 0)], ["ymix"])
            cp(ucar[:, l], ubuf[:, :, NS:NS + 2], ["ubuf"], ["ucar"])
            P.mark("conv_done")
            for m in range(8):
                w, wk = wload(wmo[l, m], 1280)
                wv = w[:, 0:1280].rearrange("p (s n) -> p s n", s=10)
                po, pok = nps()
                for s_ in range(6):
                    mm(po[:], wv[:, s_, :], ymix[:, s_, :], s_ == 0, False, [wk, "ymix"], [pok])
                for hd in range(4):
                    mm(po[:], wv[0:64, 6 + hd, :], yr[:, hd, :], False, hd == 3, [wk, ("yr", hd)], [pok])
                stt(h[:, m, t0:t0 + NS], po[:], coef[:, l, 24 + m:25 + m], h[:, m, t0:t0 + NS], ALU.mult, ALU.add,
                    [pok, "coef", ("h", m)], [("h", m)])

        xv = xT.rearrange("(c p) t -> p c t", p=128)
        ov = outT.rearrange("(c p) t -> p c t", p=128)
        for p_ in range(NP):
            tb = p_ * NT
            for c in range(8):
                P.dma("sp", lambda e, c=c, tb=tb: e.dma_start(out=h[:, c, :], in_=xv[:, c, tb:tb + NT]), ("hin", c), writes=[("h", c)])
            for l in range(L):
                P.mark("ffn1_start")
                if "ffn1" in PARTS:
                    ffn(l, 0, w1i, w1o)
                P.barrier()
                if PARTS & {"att", "rwkv", "conv", "mix"}:
                    for st in range(NT // NS):
                        mixer(l, st, first=(p_ == 0 and st == 0))
                P.barrier()
                P.mark("ffn2_start")
                if "ffn2" in PARTS:
                    ffn(l, 2, w2i, w2o)
                P.barrier()
            for c in range(8):
                P.dma("sp", lambda e, c=c, tb=tb: e.dma_start(out=ov[:, c, tb:tb + NT], in_=h[:, c, :]), ("hout", c), reads=[("h", c)], is_out=True)
        P.mark("end")
        P.emit()
        if os.environ.get("MK_MARKS"):
            import json
            cum = np.concatenate([[0], np.cumsum(P.pe_w)])
            json.dump([(n, int(cum[i])) for n, i in P.marks], open(os.environ["MK_MARKS"], "w"))
    return nc


_CACHE = {}


def run(inputs, L, T, ncores=8):
    lay = host_layout(inputs, L)
    key = (L, T)
    if key not in _CACHE:
        _CACHE[key] = build_program(L, T)
    nc = _CACHE[key]
    x = np.asarray(inputs["x"], np.float32)
    c = np.asarray(inputs["c"], np.float32)
    in_maps = []
    for b in range(ncores):
        m = dict(lay)
        m["xT"] = np.ascontiguousarray(x[b, :T].T)
        m["cT"] = np.ascontiguousarray(c[b].reshape(8, 128).T)
        in_maps.append(m)
    res = run_bass_kernel_spmd(nc, in_maps, core_ids=list(range(ncores)))
    return np.stack([np.ascontiguousarray(r["outT"].T) for r in res.results], axis=0)


def kernel(**inputs):
    inputs = {k: np.asarray(v) for k, v in inputs.items()}
    return run(inputs, 4, 4096).astype(np.float32)
```
